# Optimizing a Trainium2 kernel written in Bass

```python
import jax
import jax.numpy as jnp
from jax import lax
import numpy as np

D_MODEL = 1024
BATCH = 8
SEQ = 4096
DEPTH = 2

N_EVEN = (DEPTH + 1) // 2
N_ODD = DEPTH // 2
ROPE_THETA = 10000.0
RMS_EPS = 1e-6
NEG_INF = -1e30
FORCE_SCORE = 1e4

NSA_HEADS = 8
NSA_KV_GROUPS = 2
NSA_HPG = NSA_HEADS // NSA_KV_GROUPS
NSA_HEAD_DIM = 64
NSA_WIDTH = NSA_HEADS * NSA_HEAD_DIM
NSA_KV_WIDTH = NSA_KV_GROUPS * NSA_HEAD_DIM
CMP_BLOCK = 32
CMP_STRIDE = 16
CMP_HIDDEN = 2 * NSA_HEAD_DIM
SLC_BLOCK = 64
SLC_TOPK = 16
WINDOW = 512
NSA_QBLOCK = 64

CONV_WIDTH = D_MODEL - NSA_WIDTH
CONV_K = 3

IN_A_SIZES = (NSA_WIDTH, NSA_KV_WIDTH, NSA_KV_WIDTH, NSA_KV_WIDTH, NSA_KV_WIDTH, NSA_KV_WIDTH, NSA_KV_WIDTH,
              3 * NSA_HEADS, NSA_WIDTH, CONV_WIDTH, CONV_WIDTH, CONV_WIDTH, CONV_WIDTH)
IN_A_WIDTH = 2 * NSA_WIDTH + 6 * NSA_KV_WIDTH + 3 * NSA_HEADS + 4 * CONV_WIDTH

MLA_HEADS = 8
MLA_NOPE_DIM = 128
MLA_ROPE_DIM = 64
MLA_V_DIM = 128
MLA_Q_RANK = 256
MLA_KV_RANK = 256
MLA_WIDTH = MLA_HEADS * MLA_V_DIM
MLA_QBLOCK = 128
IN_C_SIZES = (MLA_Q_RANK, MLA_KV_RANK, MLA_ROPE_DIM, MLA_WIDTH)
IN_C_WIDTH = MLA_Q_RANK + MLA_KV_RANK + MLA_ROPE_DIM + MLA_WIDTH

kernel_name = 'hybrid_nsa_shortconv_mla'


def _split(t, sizes):
    return jnp.split(t, np.cumsum(sizes)[:-1].tolist(), axis=-1)


def rms_norm(x, g):
    xf = x.astype(jnp.float32)
    y = xf * lax.rsqrt(jnp.mean(xf * xf, axis=-1, keepdims=True) + RMS_EPS)
    return (y * g.astype(jnp.float32)).astype(x.dtype)


def rope(x, pos):
    half = x.shape[-1] // 2
    inv_freq = ROPE_THETA ** (-jnp.arange(half, dtype=jnp.float32) / half)
    ang = pos.astype(jnp.float32)[..., None] * inv_freq
    cos = jnp.cos(ang)[:, :, None, :]
    sin = jnp.sin(ang)[:, :, None, :]
    x1 = x[..., :half].astype(jnp.float32)
    x2 = x[..., half:].astype(jnp.float32)
    out = jnp.concatenate([x1 * cos - x2 * sin, x2 * cos + x1 * sin], axis=-1)
    return out.astype(x.dtype)


def masked_softmax(s, valid):
    s = jnp.where(valid, s.astype(jnp.float32), NEG_INF)
    return jnp.where(valid, jax.nn.softmax(s, axis=-1), 0.0)


def nsa_attention(q, k_cmp_raw, v_cmp_raw, k_slc, v_slc, k_win, v_win, gates, positions,
                  pe_k, pe_v, w_ck1, w_ck2, w_cv1, w_cv2):
    B, S = q.shape[:2]
    G, HPG, hd, QB = NSA_KV_GROUPS, NSA_HPG, NSA_HEAD_DIM, NSA_QBLOCK
    scale = hd ** -0.5
    q = rope(q, positions).reshape(B, S, G, HPG, hd)
    k_slc = rope(k_slc, positions)
    k_win = rope(k_win, positions)

    n_cmp = (S - CMP_BLOCK) // CMP_STRIDE + 1
    cmp_starts = jnp.arange(n_cmp) * CMP_STRIDE
    cmp_idx = cmp_starts[:, None] + jnp.arange(CMP_BLOCK)[None, :]
    cmp_ends = cmp_starts + CMP_BLOCK - 1

    def compress(raw, pe, w1, w2):
        blk = raw[:, cmp_idx] + pe[None, None, :, None, :]
        blk = blk.transpose(0, 1, 3, 2, 4).reshape(B, n_cmp, G, CMP_BLOCK * hd)
        return jax.nn.silu(blk @ w1) @ w2

    k_cmp = rope(compress(k_cmp_raw, pe_k, w_ck1, w_ck2), positions[:, cmp_ends])
    v_cmp = compress(v_cmp_raw, pe_v, w_cv1, w_cv2)

    n_slc = S // SLC_BLOCK
    n_top = min(SLC_TOPK, n_slc)
    slc_starts = jnp.arange(n_slc) * SLC_BLOCK
    overlap = ((cmp_starts[:, None] < slc_starts[None, :] + SLC_BLOCK) &
               (cmp_starts[:, None] + CMP_BLOCK > slc_starts[None, :])).astype(jnp.float32)
    k_blocks = k_slc.reshape(B, n_slc, SLC_BLOCK, G, hd).transpose(0, 3, 1, 2, 4)
    v_blocks = v_slc.reshape(B, n_slc, SLC_BLOCK, G, hd).transpose(0, 3, 1, 2, 4)

    k_pad = jnp.pad(k_win, ((0, 0), (WINDOW, 0), (0, 0), (0, 0)))
    v_pad = jnp.pad(v_win, ((0, 0), (WINDOW, 0), (0, 0), (0, 0)))

    gates = gates.reshape(B, S, G, HPG, 3)
    bi = jnp.arange(B)[:, None, None, None]
    gi = jnp.arange(G)[None, :, None, None]
    blk_id = jnp.arange(n_slc)[None, :]

    def block(qi):
        t0 = qi * QB
        tq = t0 + jnp.arange(QB)
        qb = lax.dynamic_slice_in_dim(q, t0, QB, axis=1)
        gb = lax.dynamic_slice_in_dim(gates, t0, QB, axis=1)

        s_c = jnp.einsum('bqghd,bcgd->bghqc', qb, k_cmp) * scale
        p_c = masked_softmax(s_c, cmp_ends[None, :] <= tq[:, None])
        o_c = jnp.einsum('bghqc,bcgd->bqghd', p_c.astype(v_cmp.dtype), v_cmp)

        imp = jnp.einsum('bghqc,cn->bgqn', p_c, overlap)
        blk_valid = slc_starts[None, :] <= tq[:, None]
        cur = (tq // SLC_BLOCK)[:, None]
        forced = (blk_id == 0) | (blk_id == cur) | (blk_id == cur - 1)
        imp = jnp.where(blk_valid, jnp.where(forced, FORCE_SCORE, imp), -1.0)
        sel = lax.top_k(imp, n_top)[1]
        ks = k_blocks[bi, gi, sel]
        vs = v_blocks[bi, gi, sel]
        tok = sel[..., None] * SLC_BLOCK + jnp.arange(SLC_BLOCK)
        valid_s = (tok <= tq[None, None, :, None, None]).reshape(B, G, 1, QB, n_top * SLC_BLOCK)
        s_s = jnp.einsum('bqghd,bgqnld->bghqnl', qb, ks) * scale
        p_s = masked_softmax(s_s.reshape(B, G, HPG, QB, n_top * SLC_BLOCK), valid_s)
        p_s = p_s.reshape(B, G, HPG, QB, n_top, SLC_BLOCK)
        o_s = jnp.einsum('bghqnl,bgqnld->bqghd', p_s.astype(vs.dtype), vs)

        kw = lax.dynamic_slice_in_dim(k_pad, t0, WINDOW + QB, axis=1)
        vw = lax.dynamic_slice_in_dim(v_pad, t0, WINDOW + QB, axis=1)
        kpos = t0 - WINDOW + jnp.arange(WINDOW + QB)
        valid_w = ((kpos[None, :] <= tq[:, None]) & (kpos[None, :] > tq[:, None] - WINDOW) &
                   (kpos[None, :] >= 0))
        s_w = jnp.einsum('bqghd,bkgd->bghqk', qb, kw) * scale
        p_w = masked_softmax(s_w, valid_w)
        o_w = jnp.einsum('bghqk,bkgd->bqghd', p_w.astype(vw.dtype), vw)

        o = gb[..., 0:1] * o_c + gb[..., 1:2] * o_s + gb[..., 2:3] * o_w
        return o.reshape(B, QB, G * HPG * hd)

    out = lax.map(block, jnp.arange(S // QB))
    return out.transpose(1, 0, 2, 3).reshape(B, S, NSA_WIDTH)


def short_conv(b_gate, c_gate, h, w):
    S = h.shape[1]
    u = c_gate * h
    up = jnp.pad(u, ((0, 0), (CONV_K - 1, 0), (0, 0)))
    y = sum(w[k] * up[:, k:k + S] for k in range(CONV_K))
    return b_gate * y


def mla_attention(c_q, c_kv, k_rope_raw, positions, q_norm, kv_norm, w_uq, w_ukv):
    B, S = c_q.shape[:2]
    H, QB = MLA_HEADS, MLA_QBLOCK
    q = (rms_norm(c_q, q_norm) @ w_uq).reshape(B, S, H, MLA_NOPE_DIM + MLA_ROPE_DIM)
    q = jnp.concatenate([q[..., :MLA_NOPE_DIM], rope(q[..., MLA_NOPE_DIM:], positions)], axis=-1)
    kv = (rms_norm(c_kv, kv_norm) @ w_ukv).reshape(B, S, H, MLA_NOPE_DIM + MLA_V_DIM)
    k_nope, v = kv[..., :MLA_NOPE_DIM], kv[..., MLA_NOPE_DIM:]
    k_pe = rope(k_rope_raw[:, :, None, :], positions)
    k = jnp.concatenate([k_nope, jnp.broadcast_to(k_pe, (B, S, H, MLA_ROPE_DIM))], axis=-1)
    scale = (MLA_NOPE_DIM + MLA_ROPE_DIM) ** -0.5
    kpos = jnp.arange(S)

    def block(qi):
        t0 = qi * QB
        qb = lax.dynamic_slice_in_dim(q, t0, QB, axis=1)
        s = jnp.einsum('bqhd,bkhd->bhqk', qb, k) * scale
        p = masked_softmax(s, kpos[None, :] <= (t0 + jnp.arange(QB))[:, None])
        return jnp.einsum('bhqk,bkhd->bqhd', p.astype(v.dtype), v).reshape(B, QB, MLA_WIDTH)

    out = lax.map(block, jnp.arange(S // QB))
    return out.transpose(1, 0, 2, 3).reshape(B, S, MLA_WIDTH)


def even_layer(x, positions, norm_g, w_in, pe_k, pe_v, w_ck1, w_ck2, w_cv1, w_cv2, conv_w, w_out):
    B, S = x.shape[:2]
    proj = rms_norm(x, norm_g) @ w_in
    (q, kc, vc, ksl, vsl, kw, vw, gate_logits, gate_a,
     cb, cc, ch, gate_b) = _split(proj, IN_A_SIZES)
    heads = lambda t, n: t.reshape(B, S, n, NSA_HEAD_DIM)
    o_a = nsa_attention(heads(q, NSA_HEADS), heads(kc, NSA_KV_GROUPS), heads(vc, NSA_KV_GROUPS),
                        heads(ksl, NSA_KV_GROUPS), heads(vsl, NSA_KV_GROUPS),
                        heads(kw, NSA_KV_GROUPS), heads(vw, NSA_KV_GROUPS),
                        jax.nn.sigmoid(gate_logits).reshape(B, S, NSA_HEADS, 3), positions,
                        pe_k, pe_v, w_ck1, w_ck2, w_cv1, w_cv2)
    o_b = short_conv(cb, cc, ch, conv_w)
    mixed = jnp.concatenate([jax.nn.silu(gate_a) * o_a, jax.nn.silu(gate_b) * o_b], axis=-1)
    return x + mixed @ w_out


def odd_layer(x, positions, norm_g, w_in, q_norm, kv_norm, w_uq, w_ukv, w_out):
    proj = rms_norm(x, norm_g) @ w_in
    c_q, c_kv, k_rope_raw, gate = _split(proj, IN_C_SIZES)
    o = mla_attention(c_q, c_kv, k_rope_raw, positions, q_norm, kv_norm, w_uq, w_ukv)
    return x + (jax.nn.silu(gate) * o) @ w_out


def _normal(key, shape, scale):
    return jax.random.normal(key, shape, jnp.float32) * scale


def setup_inputs(seed: int = 0) -> dict:
    key = jax.random.key(seed)
    ks = jax.random.split(key, 20)
    hd = NSA_HEAD_DIM
    x = jax.random.normal(ks[0], (BATCH, SEQ, D_MODEL), jnp.float32)
    offset = jax.random.randint(ks[1], (BATCH, 1), 0, 1024, dtype=jnp.int32)
    positions = (offset + jnp.arange(SEQ, dtype=jnp.int32)[None, :]).astype(jnp.int32)
    return {
        'x': x,
        'positions': positions,
        'a_norm': 1.0 + _normal(ks[2], (N_EVEN, D_MODEL), 0.02),
        'a_w_in': _normal(ks[3], (N_EVEN, D_MODEL, IN_A_WIDTH), D_MODEL ** -0.5),
        'a_pe_k': _normal(ks[4], (N_EVEN, CMP_BLOCK, hd), 0.02),
        'a_pe_v': _normal(ks[5], (N_EVEN, CMP_BLOCK, hd), 0.02),
        'a_w_ck1': _normal(ks[6], (N_EVEN, CMP_BLOCK * hd, CMP_HIDDEN), (CMP_BLOCK * hd) ** -0.5),
        'a_w_ck2': _normal(ks[7], (N_EVEN, CMP_HIDDEN, hd), CMP_HIDDEN ** -0.5),
        'a_w_cv1': _normal(ks[8], (N_EVEN, CMP_BLOCK * hd, CMP_HIDDEN), (CMP_BLOCK * hd) ** -0.5),
        'a_w_cv2': _normal(ks[9], (N_EVEN, CMP_HIDDEN, hd), CMP_HIDDEN ** -0.5),
        'a_conv_w': _normal(ks[10], (N_EVEN, CONV_K, CONV_WIDTH), CONV_K ** -0.5),
        'a_w_out': _normal(ks[11], (N_EVEN, NSA_WIDTH + CONV_WIDTH, D_MODEL), (NSA_WIDTH + CONV_WIDTH) ** -0.5),
        'c_norm': 1.0 + _normal(ks[12], (N_ODD, D_MODEL), 0.02),
        'c_w_in': _normal(ks[13], (N_ODD, D_MODEL, IN_C_WIDTH), D_MODEL ** -0.5),
        'c_q_norm': 1.0 + _normal(ks[14], (N_ODD, MLA_Q_RANK), 0.02),
        'c_kv_norm': 1.0 + _normal(ks[15], (N_ODD, MLA_KV_RANK), 0.02),
        'c_w_uq': _normal(ks[16], (N_ODD, MLA_Q_RANK, MLA_HEADS * (MLA_NOPE_DIM + MLA_ROPE_DIM)), MLA_Q_RANK ** -0.5),
        'c_w_ukv': _normal(ks[17], (N_ODD, MLA_KV_RANK, MLA_HEADS * (MLA_NOPE_DIM + MLA_V_DIM)), MLA_KV_RANK ** -0.5),
        'c_w_out': _normal(ks[18], (N_ODD, MLA_WIDTH, D_MODEL), MLA_WIDTH ** -0.5),
        'final_norm': 1.0 + _normal(ks[19], (D_MODEL,), 0.02),
    }


def reference(x, positions, a_norm, a_w_in, a_pe_k, a_pe_v, a_w_ck1, a_w_ck2, a_w_cv1, a_w_cv2,
              a_conv_w, a_w_out, c_norm, c_w_in, c_q_norm, c_kv_norm, c_w_uq, c_w_ukv, c_w_out,
              final_norm):
    for i in range(DEPTH):
        j = i // 2
        if i % 2 == 0:
            x = even_layer(x, positions, a_norm[j], a_w_in[j], a_pe_k[j], a_pe_v[j],
                           a_w_ck1[j], a_w_ck2[j], a_w_cv1[j], a_w_cv2[j], a_conv_w[j], a_w_out[j])
        else:
            x = odd_layer(x, positions, c_norm[j], c_w_in[j], c_q_norm[j], c_kv_norm[j],
                          c_w_uq[j], c_w_ukv[j], c_w_out[j])
    return rms_norm(x, final_norm)
```

```python
import math
from contextlib import ExitStack

import numpy as np
import concourse.bass as bass
import concourse.mybir as mybir
from concourse.bass_utils import run_bass_kernel_spmd

F32 = mybir.dt.float32
BF16 = mybir.dt.bfloat16
I32 = mybir.dt.int32
F32R = mybir.dt.float32r
ALU = mybir.AluOpType
AF = mybir.ActivationFunctionType

ENGS = ("pe", "act", "dve", "pool", "sp")

S = 4096
DM = 1024
TT = 512
NT = S // TT
NEG = -30000.0


class Prog:
    def __init__(self, nc, strict=("act", "dve", "pool")):
        self.nc = nc
        self.ops = []
        self.last_w = {}
        self.readers = {}
        self.strict = set(strict)
        self.es = ExitStack()
        self.pes = None
        self.dma_keys = []
        self.last_real = {}
        self.dma_since = []
        self.bank_rr = 0
        self.last_dma_by_key = {}
        self.deferred = None
        self.max_ops = None
        self.n_real = 0
        self.last_desc = None

    def sbuf(self, name, shape, dt):
        return self.es.enter_context(self.nc.sbuf_tensor(name, list(shape), dt))

    def psum(self, name, shape, dt=F32):
        return self.es.enter_context(self.nc.psum_tensor(name, list(shape), dt))

    def phase_begin(self):
        self.pes = ExitStack()
        self.phase_no = getattr(self, "phase_no", 0) + 1

    def palloc(self, name, shape, dt):
        return self.pes.enter_context(self.nc.sbuf_tensor("%s_p%d" % (name, self.phase_no), list(shape), dt))

    def phase_end(self):
        self.barrier()
        self.pes.close()
        self.pes = None

    def defer_begin(self):
        self.deferred = []

    def defer_end(self):
        d = self.deferred
        self.deferred = None
        return d

    def replay(self, lst):
        for a in lst:
            self.add(*a)

    def add(self, eng, fn, reads=(), writes=(), dma=None, ndma=1, xdeps=None, real=True):
        if self.deferred is not None:
            self.deferred.append((eng, fn, tuple(reads), tuple(writes), dma, ndma, xdeps, real))
            return None
        if real and self.max_ops is not None:
            if self.n_real >= self.max_ops:
                return None
            self.n_real += 1
            self.last_desc = (eng, list(writes), dma)
        i = len(self.ops)
        deps = set()
        if xdeps is not None:
            deps.update(xdeps)
        for t in reads:
            if t in self.last_w:
                deps.add(self.last_w[t])
        for t in writes:
            if t in self.last_w:
                deps.add(self.last_w[t])
            for r in self.readers.get(t, {}).values():
                if isinstance(r, list):
                    deps.update(r)
                else:
                    deps.add(r)
        op = dict(i=i, eng=eng, fn=fn, dma=dma, ndma=ndma, deps=[], signal=False)
        deps = {(self.last_dma_by_key[self.ops[d]["dma"]] if self.ops[d]["dma"] is not None else d) for d in deps}
        for d in sorted(deps):
            A = self.ops[d]
            if A["dma"] is None and A["eng"] == eng and eng not in self.strict:
                continue
            op["deps"].append(d)
            A["signal"] = True
        for t in reads:
            rd = self.readers.setdefault(t, {})
            if dma is None:
                rd[eng] = i
            else:
                rd.setdefault("dma", []).append(i)
        for t in writes:
            self.last_w[t] = i
            self.readers[t] = {}
        if dma is not None:
            if dma not in self.dma_keys:
                self.dma_keys.append(dma)
            self.dma_since.append(i)
            self.last_dma_by_key[dma] = i
        elif real:
            self.last_real[eng] = i
        self.ops.append(op)
        return i

    def barrier(self):
        lr = dict(self.last_real)
        dmas = list(self.dma_since)
        for e in ENGS:
            x = [v for k, v in lr.items() if k != e] + dmas
            self.add(e, lambda h: None, xdeps=x, real=False)
        self.dma_since = []
        self.last_w = {}
        self.readers = {}

    def pe(self, fn, r=(), w=()):
        return self.add("pe", fn, r, w)

    def act(self, fn, r=(), w=()):
        return self.add("act", fn, r, w)

    def dve(self, fn, r=(), w=()):
        return self.add("dve", fn, r, w)

    def pool(self, fn, r=(), w=()):
        return self.add("pool", fn, r, w)

    def dma(self, fn, key, r=(), w=(), ndma=1, eng="sp"):
        return self.add(eng, fn, r, w, dma=key, ndma=ndma)

    def emit(self):
        nc = self.nc
        es = self.es
        sems = {}
        for e in ENGS:
            sems[("eng", e)] = es.enter_context(nc.semaphore("s_" + e))
        for n, k in enumerate(self.dma_keys):
            sems[("dma", k)] = es.enter_context(nc.semaphore("d%d" % n))
        eng_count = {e: 0 for e in ENGS}
        dma_count = {}
        for op in self.ops:
            if op["dma"] is None:
                if op["signal"]:
                    eng_count[op["eng"]] += 1
                    op["sem"] = ("eng", op["eng"])
                    op["val"] = eng_count[op["eng"]]
            else:
                k = op["dma"]
                dma_count[k] = dma_count.get(k, 0) + 16 * op["ndma"]
                op["sem"] = ("dma", k)
                op["val"] = dma_count[k]
        self.stats = dict(eng_count=dict(eng_count), n_ops=len(self.ops), n_sems=len(sems))
        per_eng = {e: [op for op in self.ops if op["eng"] == e] for e in ENGS}
        ops = self.ops

        def run(e, h):
            waited = {}
            nwait = 0
            for op in per_eng[e]:
                need = {}
                for d in op["deps"]:
                    A = ops[d]
                    s, v = A["sem"], A["val"]
                    if need.get(s, 0) < v:
                        need[s] = v
                for s, v in need.items():
                    if waited.get(s, 0) < v:
                        h.wait_ge(sems[s], v)
                        waited[s] = v
                        nwait += 1
                if op["dma"] is None:
                    ins = op["fn"](h)
                    if op["signal"]:
                        ins.then_inc(sems[op["sem"]], 1)
                else:
                    op["fn"](h, sems[op["sem"]])
            self.stats["waits_" + e] = nwait

        with nc.Block() as block:
            @block.tensor
            def _(h):
                run("pe", h)

            @block.scalar
            def _(h):
                run("act", h)

            @block.vector
            def _(h):
                run("dve", h)

            @block.gpsimd
            def _(h):
                run("pool", h)

            @block.sync
            def _(h):
                run("sp", h)

    def close(self):
        self.es.close()


class Builder:
    def __init__(self, mode="full", dbg=()):
        self.mode = mode
        self.nc = nc = bass.Bass("TRN2", target_bir_lowering=False)
        self.P = Prog(nc)
        self.D = {}
        self.dbg = dbg
        self.stop_after = None
        self.wreg = {}
        self._uid = 0

    def din(self, name, shape, dt=F32):
        t = self.nc.dram_tensor(name, list(shape), dt, kind="ExternalInput")
        self.D[name] = t
        return t

    def dout(self, name, shape, dt=F32):
        t = self.nc.dram_tensor(name, list(shape), dt, kind="ExternalOutput")
        self.D[name] = t
        return t

    def dscr(self, name, shape, dt):
        t = self.nc.dram_tensor(name, list(shape), dt, kind="Internal")
        self.D[name] = t
        return t

    def uid(self, p="t"):
        self._uid += 1
        return "%s%d" % (p, self._uid)

    def DMA(self, out, in_, key, r=(), w=(), eng="sp"):
        self.P.dma(lambda h, s: h.dma_start(out=out, in_=in_).then_inc(s, 16), key, r, w, eng=eng)

    def MM(self, ps, lhsT, rhs, start, stop, r, w):
        self.P.pe(lambda h: h.matmul(ps, lhsT=lhsT, rhs=rhs, start=start, stop=stop), r, w)

    def ACT(self, out, in_, func, r, w, bias=None, scale=None, accum_out=None):
        kw = {}
        if bias is not None:
            kw["bias"] = bias
        if scale is not None:
            kw["scale"] = scale
        if accum_out is not None:
            kw["accum_out"] = accum_out
        self.P.act(lambda h: h.activation(out=out, in_=in_, func=func, **kw), r, w)

    def ENG(self, eng):
        return {"dve": self.P.dve, "pool": self.P.pool, "act": self.P.act}[eng]

    def CP(self, eng, out, in_, r, w):
        if eng == "act":
            self.P.act(lambda h: h.copy(out=out, in_=in_), r, w)
        else:
            self.ENG(eng)(lambda h: h.tensor_copy(out=out, in_=in_), r, w)

    def TS(self, eng, out, in0, s1, s2, op0, op1, r, w):
        if op1 is None:
            self.ENG(eng)(lambda h: h.tensor_scalar(out=out, in0=in0, scalar1=s1, scalar2=None, op0=op0), r, w)
        else:
            self.ENG(eng)(lambda h: h.tensor_scalar(out=out, in0=in0, scalar1=s1, scalar2=s2, op0=op0, op1=op1), r, w)

    def TTo(self, eng, out, in0, in1, op, r, w):
        self.ENG(eng)(lambda h: h.tensor_tensor(out=out, in0=in0, in1=in1, op=op), r, w)

    def STT(self, out, in0, scalar, in1, op0, op1, r, w, accum_out=None):
        if accum_out is None:
            self.P.dve(lambda h: h.scalar_tensor_tensor(out=out, in0=in0, scalar=scalar, in1=in1, op0=op0, op1=op1), r, w)
        else:
            self.P.dve(lambda h: h.scalar_tensor_tensor(out=out, in0=in0, scalar=scalar, in1=in1, op0=op0, op1=op1,
                                                        accum_out=accum_out), r, w)

    def MEMSET(self, eng, ap, val, w):
        self.ENG(eng)(lambda h: h.memset(ap, val), (), w)

    def wreg_add(self, name, c0, c1, tok):
        self.wreg.setdefault(name, []).append((c0, c1, tok))

    def wcover(self, name, c0, m):
        out = [t for (a, b, t) in self.wreg.get(name, []) if a < c0 + m and b > c0]
        assert out, (name, c0, m)
        return out

    def run_interleaved(self, gen, aux, nsteps):
        per = (len(aux) + nsteps - 1) // max(1, nsteps)
        k = 0
        for _ in gen:
            self.P.replay(aux[k:k + per])
            k += per
        self.P.replay(aux[k:])

    def bank(self, lo=0, hi=8):
        P = self.P
        b = lo + (P.bank_rr % (hi - lo))
        P.bank_rr += 1
        return b

    def setup_globals(self):
        P = self.P
        self.PS = [P.psum("ps%d" % b, [128, 512]) for b in range(8)]
        self.ones_f = P.sbuf("ones_f", [128, 128], F32)
        self.ones_b = P.sbuf("ones_b", [128, 128], BF16)
        self.neg1 = P.sbuf("neg1", [128, 512], F32)
        self.epsv = P.sbuf("epsv", [128, 1], F32)
        self.invf = P.sbuf("invf_sb", [128, 1], F32)
        self.cbig = P.sbuf("cbig_sb", [128, 896], BF16)
        self.wbig = P.sbuf("wbig_sb", [128, 896], BF16)
        self.MEMSET("dve", self.ones_f[:], 1.0, ["ones_f"])
        self.ones_r = P.sbuf("ones_r", [128, 128], F32R)
        self.CP("dve", self.ones_r[:], self.ones_f[:], ["ones_f"], ["ones_r"])
        self.MEMSET("dve", self.ones_b[:], 1.0, ["ones_b"])
        self.MEMSET("pool", self.neg1[:], -1.0, ["neg1"])
        self.MEMSET("dve", self.epsv[:], 1e-6, ["epsv"])
        self.DMA(self.invf[:], self.D["invf"].ap(), "g_invf", w=["invf"])
        self.identb = P.sbuf("identb", [128, 128], BF16)
        P.phase_begin()
        tmp = P.palloc("cb_tmp", [128, 896], F32)
        tmp2 = P.palloc("id_tmp", [128, 128], F32)
        self.DMA(tmp[:], self.D["cbig"].ap(), "g_cb", w=["cb_tmp"])
        self.DMA(tmp2[:], self.D["ident"].ap(), "g_id", w=["id_tmp"])
        self.CP("dve", self.identb[:], tmp2[:], ["id_tmp"], ["identb"])
        self.TS("dve", self.cbig[:], tmp[:], 30000.0, -30000.0, ALU.mult, ALU.add, ["cb_tmp"], ["cbig"])
        self.TS("dve", self.wbig[:], tmp[:], -30000.0, None, ALU.mult, None, ["cb_tmp"], ["wbig"])
        P.phase_end()

    def load_w(self, dst, src3, kcn, n, stg, dstname):
        engs = ["dve", "act"]
        for c0 in range(0, n, 256):
            m = min(256, n - c0)
            sl = self._wst % len(stg)
            e = engs[self._wst % 2]
            self._wst += 1
            self.DMA(stg[sl][:, 0:kcn, 0:m], src3[:, :, c0:c0 + m], ("wst", sl), w=[("wst", sl)])
            self.CP(e, dst[:, 0:kcn, c0:c0 + m], stg[sl][:, 0:kcn, 0:m], [("wst", sl)], [(dstname, c0)])
            self.wreg_add(dstname, c0, c0 + m, (dstname, c0))

    def make_rot(self, W, kcn, c0, r0, tok_src, tok_dst):
        self.TS("dve", W[:, 0:kcn, r0:r0 + 32], W[:, 0:kcn, c0 + 32:c0 + 64], -1.0, None, ALU.mult, None,
                tok_src, [tok_dst + ("a",)])
        self.CP("pool", W[:, 0:kcn, r0 + 32:r0 + 64], W[:, 0:kcn, c0:c0 + 32], tok_src, [tok_dst + ("b",)])
        self.wreg_add(tok_dst[0], r0, r0 + 32, tok_dst + ("a",))
        self.wreg_add(tok_dst[0], r0 + 32, r0 + 64, tok_dst + ("b",))

    def rope_tables(self, i, T, src=None, n=TT, sb_src=False):
        npart = T["npart"]
        if src is None:
            src = bass.AP(self.D["pos"], TT * i, [[0, npart], [1, TT]])
        ti, f0, f1, f2, f3 = (T[k] for k in ("ti", "f0", "f1", "f2", "f3"))
        slot = i % len(T["cos"])
        cs, sn = T["cos"][slot], T["sin"][slot]
        if sb_src:
            self.CP("dve", ti[:, 0:n], src, ["posfull"], ["rt_ti"])
        else:
            self.DMA(ti[:, 0:n], src, "rt_ti", w=["rt_ti"])
        self.CP("dve", f0[:, 0:n], ti[:, 0:n], ["rt_ti"], ["rt_f0"])
        self.TS("dve", f1[:, 0:n], f0[:, 0:n], self.invf[0:npart, 0:1], None, ALU.mult, None, ["rt_f0", "invf"], ["rt_f1"])
        self.TS("dve", f0[:, 0:n], f1[:, 0:n], float(1.0 / (2 * math.pi)), 0.5, ALU.mult, ALU.add, ["rt_f1"], ["rt_f0"])
        self.CP("dve", ti[:, 0:n], f0[:, 0:n], ["rt_f0"], ["rt_ti"])
        self.CP("dve", f0[:, 0:n], ti[:, 0:n], ["rt_ti"], ["rt_f0"])
        C1 = 6.28125
        C2 = float(np.float32(2 * math.pi - 6.28125))
        self.STT(f2[:, 0:n], f0[:, 0:n], -C1, f1[:, 0:n], ALU.mult, ALU.add, ["rt_f0", "rt_f1"], ["rt_f2"])
        self.STT(f2[:, 0:n], f0[:, 0:n], -C2, f2[:, 0:n], ALU.mult, ALU.add, ["rt_f0", "rt_f2"], ["rt_f2"])
        B = 3.1415925
        TWO_PI = 2 * math.pi

        def wrap(dst, shift, tag):
            self.TS("dve", dst[:, 0:n], f2[:, 0:n], float(shift), None, ALU.add, None, ["rt_f2"], [tag])
            self.TS("dve", f0[:, 0:n], dst[:, 0:n], math.pi, -TWO_PI, ALU.is_gt, ALU.mult, [tag], ["rt_f0"])
            self.TTo("dve", dst[:, 0:n], dst[:, 0:n], f0[:, 0:n], ALU.add, [tag, "rt_f0"], [tag])
            self.TS("dve", f0[:, 0:n], dst[:, 0:n], -math.pi, TWO_PI, ALU.is_lt, ALU.mult, [tag], ["rt_f0"])
            self.TTo("dve", dst[:, 0:n], dst[:, 0:n], f0[:, 0:n], ALU.add, [tag, "rt_f0"], [tag])
            self.TS("dve", dst[:, 0:n], dst[:, 0:n], B, -B, ALU.min, ALU.max, [tag], [tag])

        wrap(f1, 0.0, "rt_f1")
        self.ACT(sn[:, 0:n], f1[:, 0:n], AF.Sin, ["rt_f1"], [("rt_sin", slot)])
        wrap(f3, math.pi / 2, "rt_f3")
        self.ACT(cs[:, 0:n], f3[:, 0:n], AF.Sin, ["rt_f3"], [("rt_cos", slot)])
        return cs, sn, [("rt_cos", slot), ("rt_sin", slot)]

    def alloc_rope_scratch(self, npart=64, nslots=1):
        P = self.P
        T = {"npart": npart}
        T["ti"] = P.palloc("rt_ti", [npart, TT], I32)
        for k in ("f0", "f1", "f2", "f3"):
            T[k] = P.palloc("rt_" + k, [npart, TT], F32)
        T["cos"] = [P.palloc("rt_cos%d" % k, [npart, TT], F32) for k in range(nslots)]
        T["sin"] = [P.palloc("rt_sin%d" % k, [npart, TT], F32) for k in range(nslots)]
        T["t1"] = [P.palloc("rp_t1_%d" % s, [npart, TT], F32) for s in range(1)]
        T["t2"] = [P.palloc("rp_t2_%d" % s, [npart, TT], F32) for s in range(1)]
        self._rp = 0
        return T

    def rope_apply(self, out, psa, psb, cs, sn, T, r, w, n=TT, npart=None):
        t1, t2 = T["t1"][0], T["t2"][0]
        np_ = npart if npart is not None else 64
        self.TTo("dve", t1[0:np_, 0:n], psa, cs[0:np_, 0:n], ALU.mult, r, ["rp1"])
        self.TTo("dve", t2[0:np_, 0:n], psb, sn[0:np_, 0:n], ALU.mult, r, ["rp2"])
        self.TTo("pool", out, t1[0:np_, 0:n], t2[0:np_, 0:n], ALU.add, ["rp1", "rp2"], w)

    def rstd_tile(self, xt, kcn, inv_n, T, rtok):
        b = 7
        psN = self.PS[b]
        for c in range(kcn):
            sl = self._sq % 2
            self._sq += 1
            sq = T["sq"][sl]
            self.ACT(sq[:], xt[:, c, :], AF.Square, rtok, [("sq", sl)])
            self.MM(psN[:], self.ones_r[:], sq[:], c == 0, c == kcn - 1, ["ones_r", ("sq", sl)], [("ps", b)])
        lnt = T["lnt"]
        rstd = T["rstd"]
        self.ACT(lnt[:], psN[:], AF.Ln, [("ps", b), "epsv"], ["lnt"], bias=self.epsv[:, 0:1], scale=float(inv_n))
        self.ACT(rstd[:], lnt[:], AF.Exp, ["lnt"], ["rstd"], scale=-0.5)
        return rstd

    def alloc_norm_scratch(self):
        P = self.P
        T = {}
        T["sq"] = [P.palloc("sq%d" % s, [128, TT], F32R) for s in range(2)]
        T["lnt"] = P.palloc("lnt", [128, TT], F32)
        T["rstd"] = P.palloc("rstd", [128, TT], F32)
        self._sq = 0
        return T

    def norm_apply(self, out, xt, kcn, gvec, rstd, rtok, wtok, gtok):
        for c in range(kcn):
            self.STT(out[:, c, :], xt[:, c, :], gvec[:, c:c + 1], rstd[:], ALU.mult, ALU.mult,
                     list(rtok) + ["rstd", gtok], [wtok + (c,)])

    def phase_C1(self):
        P = self.P
        D = self.D
        P.phase_begin()
        self._wst = 0
        Wc = P.palloc("Wc", [128, 8, 1664], BF16)
        Wuq = P.palloc("Wuq", [128, 2, 2048], BF16)
        Wukv = P.palloc("Wukv", [128, 2, 2048], BF16)
        stg = [P.palloc("wstg%d" % s, [128, 8, 256], F32) for s in range(4)]
        cn = P.palloc("cn", [128, 8], F32)
        qn_g = P.palloc("qn_g", [128, 2], F32)
        kvn_g = P.palloc("kvn_g", [128, 2], F32)
        self.DMA(cn[:], D["c_norm"].ap(), "cn", w=["cn"])
        self.DMA(qn_g[:], D["c_q_norm"].ap(), "qn_g", w=["qn_g"])
        self.DMA(kvn_g[:], D["c_kv_norm"].ap(), "kvn_g", w=["kvn_g"])
        NS = self.alloc_norm_scratch()
        RS = self.alloc_rope_scratch(64, 2)
        xt = [P.palloc("xt%d" % s, [128, 8, TT], F32) for s in range(2)]
        xn2 = [P.palloc("xn%d" % s_, [128, 8, TT], BF16) for s_ in range(2)]
        cqf = P.palloc("cqf", [128, 2, TT], F32)
        ckvf = P.palloc("ckvf", [128, 2, TT], F32)
        cqn = P.palloc("cqn", [128, 2, TT], BF16)
        ckvn = P.palloc("ckvn", [128, 2, TT], BF16)
        kpe_s = P.palloc("kpe_s", [64, TT], BF16)
        gst = P.palloc("gst", [128, 8, TT], BF16)
        qst = P.palloc("qst", [128, 8, TT], BF16)
        qrst = P.palloc("qrst", [64, 8, TT], BF16)
        kst = P.palloc("kst", [128, 8, TT], BF16)
        vst = [P.palloc("vst%d" % s, [128, 4, 128], BF16) for s in range(2)]

        x1v = D["x1T"].ap().rearrange("(c p) t -> p c t", p=128)
        gTv = D["gT1"].ap().rearrange("(c p) t -> p c t", p=128)
        qnv = D["qn"].ap().rearrange("h p t -> p h t")
        qrv = D["qr"].ap().rearrange("h p t -> p h t")
        knv = D["kn"].ap().rearrange("h p t -> p h t")

        def load_x(i):
            sl = i % 2
            self.DMA(xt[sl][:], x1v[:, :, i * TT:(i + 1) * TT], ("xt", sl), w=[("xt", sl)])

        load_x(0)
        load_x(1)
        evs = ["act", "dve"]
        nev = 0

        def pre(i):
            sl = i % 2
            xtok = [("xt", sl)]
            rstd = self.rstd_tile(xt[sl], 8, 1.0 / DM, NS, xtok)
            self.norm_apply(xn2[sl], xt[sl], 8, cn, rstd, xtok, ("xn", sl), "cn")

        def body(i):
            nonlocal nev
            sl = i % 2
            xn = xn2[sl]
            tsl = slice(i * TT, (i + 1) * TT)
            xnR = [("xn", sl, c) for c in range(8)]
            cs, sn, rtoks = ropeT[i]

            def proj8(ps_ap, col0, m, wtok):
                for kc in range(8):
                    self.MM(ps_ap, Wc[:, kc, col0:col0 + m], xn[:, kc, :], kc == 0, kc == 7, self.wcover("Wc", col0, m) + xnR, wtok)

            for (dstf, col0, nm) in ((cqf, 0, "cqf"), (ckvf, 256, "ckvf")):
                for j in range(2):
                    b = self.bank(0, 6)
                    proj8(self.PS[b][:], col0 + 128 * j, 128, [("ps", b)])
                    self.CP("act", dstf[:, j, :], self.PS[b][:], [("ps", b)], [(nm, j)])
            for (dstf, dstn, gv, nm, gtok) in ((cqf, cqn, qn_g, "cqf", "qn_g"), (ckvf, ckvn, kvn_g, "ckvf", "kvn_g")):
                ftok = [(nm, 0), (nm, 1)]
                rstd = self.rstd_tile(dstf, 2, 1.0 / 256, NS, ftok)
                self.norm_apply(dstn, dstf, 2, gv, rstd, ftok, (nm + "n",), gtok)
            cqnR = [("cqfn", 0), ("cqfn", 1)]
            ckvnR = [("ckvfn", 0), ("ckvfn", 1)]
            ba = self.bank(0, 6)
            bb = self.bank(0, 6)
            proj8(self.PS[ba][0:64, :], 512, 64, [("ps", ba)])
            proj8(self.PS[bb][0:64, :], 1600, 64, [("ps", bb)])
            self.rope_apply(kpe_s[:], self.PS[ba][0:64, :], self.PS[bb][0:64, :], cs, sn, RS,
                            [("ps", ba), ("ps", bb)] + rtoks, ["kpe_s"])
            self.DMA(D["kpe"].ap()[:, tsl], kpe_s[:], "kpe_s", r=["kpe_s"], w=[("kpe_d", i)])
            for c in range(8):
                b = self.bank(0, 6)
                proj8(self.PS[b][:], 576 + 128 * c, 128, [("ps", b)])
                self.ACT(gst[:, c, :], self.PS[b][:], AF.Silu, [("ps", b)], [("gst", c)])
            self.DMA(gTv[:, :, tsl], gst[:], "gst", r=[("gst", c) for c in range(8)], w=[("gT_d", i)])
            yield
            for h in range(8):
                b = self.bank(0, 6)
                for kc in range(2):
                    self.MM(self.PS[b][:], Wuq[:, kc, h * 192:h * 192 + 128], cqn[:, kc, :], kc == 0, kc == 1,
                            self.wcover("Wuq", h * 192, 128) + cqnR, [("ps", b)])
                self.CP(evs[nev % 2], qst[:, h, :], self.PS[b][:], [("ps", b)], [("qst", h)])
                nev += 1
                ba = self.bank(0, 6)
                bb = self.bank(0, 6)
                for kc in range(2):
                    self.MM(self.PS[ba][0:64, :], Wuq[:, kc, h * 192 + 128:h * 192 + 192], cqn[:, kc, :], kc == 0, kc == 1,
                            self.wcover("Wuq", h * 192 + 128, 64) + cqnR, [("ps", ba)])
                for kc in range(2):
                    self.MM(self.PS[bb][0:64, :], Wuq[:, kc, 1536 + 64 * h:1600 + 64 * h], cqn[:, kc, :], kc == 0, kc == 1,
                            self.wcover("Wuq", 1536 + 64 * h, 64) + cqnR, [("ps", bb)])
                self.rope_apply(qrst[:, h, :], self.PS[ba][0:64, :], self.PS[bb][0:64, :], cs, sn, RS,
                                [("ps", ba), ("ps", bb)] + rtoks, [("qrst", h)])
                b = self.bank(0, 6)
                for kc in range(2):
                    self.MM(self.PS[b][:], Wukv[:, kc, h * 256:h * 256 + 128], ckvn[:, kc, :], kc == 0, kc == 1,
                            self.wcover("Wukv", h * 256, 128) + ckvnR, [("ps", b)])
                self.CP(evs[nev % 2], kst[:, h, :], self.PS[b][:], [("ps", b)], [("kst", h)])
                nev += 1
                b = self.bank(0, 6)
                for st in range(4):
                    for kc in range(2):
                        self.MM(self.PS[b][:, st * 128:(st + 1) * 128], ckvn[:, kc, st * 128:(st + 1) * 128],
                                Wukv[:, kc, h * 256 + 128:h * 256 + 256], kc == 0, kc == 1,
                                self.wcover("Wukv", h * 256 + 128, 128) + ckvnR, [("ps", b)])
                vs = h % 2
                self.CP(evs[nev % 2], vst[vs][:].rearrange("p a d -> p (a d)"), self.PS[b][:], [("ps", b)], [("vst", vs)])
                nev += 1
                vdst = D["vS"].ap()[h].rearrange("(t p) d -> p t d", p=128)[:, 4 * i:4 * i + 4, :]
                self.DMA(vdst, vst[vs][:], ("vst", vs), r=[("vst", vs)], w=[("v_d", h, i)])
                yield
            self.DMA(qnv[:, :, tsl], qst[:], "qst", r=[("qst", h) for h in range(8)], w=[("qn_d", i)])
            self.DMA(qrv[:, :, tsl], qrst[:], "qrst", r=[("qrst", h) for h in range(8)], w=[("qr_d", i)])
            self.DMA(knv[:, :, tsl], kst[:], "kst", r=[("kst", h) for h in range(8)], w=[("kn_d", i)])
        pre(0)
        self.load_w(Wc, D["c_w_in"].ap().rearrange("(c p) n -> p c n", p=128), 8, 1600, stg, "Wc")
        self.load_w(Wuq, D["c_w_uq"].ap().rearrange("(c p) n -> p c n", p=128), 2, 1536, stg, "Wuq")
        self.load_w(Wukv, D["c_w_ukv"].ap().rearrange("(c p) n -> p c n", p=128), 2, 2048, stg, "Wukv")
        self.make_rot(Wc, 8, 512, 1600, self.wcover("Wc", 512, 64), ("Wc", "rot"))
        for h in range(8):
            c0 = h * 192 + 128
            self.make_rot(Wuq, 2, c0, 1536 + 64 * h, self.wcover("Wuq", c0, 64), ("Wuq", "rot", h))
        WcR = [("Wc", c) for c in range(0, 1600, 256)] + [("Wc", "rot", "a"), ("Wc", "rot", "b")]
        WuqR = [("Wuq", c) for c in range(0, 1536, 256)] + [("Wuq", "rot", h, x) for h in range(8) for x in "ab"]
        WukvR = [("Wukv", c) for c in range(0, 2048, 256)]

        ropeT = {0: self.rope_tables(0, RS)}
        for i in range(NT):
            gen = body(i)
            next(gen)
            if i + 2 < NT:
                load_x(i + 2)
            aux = []
            if i + 1 < NT:
                P.defer_begin()
                pre(i + 1)
                ropeT[i + 1] = self.rope_tables(i + 1, RS)
                aux = P.defer_end()
            self.run_interleaved(gen, aux, 8)
        P.phase_end()

    def phase_C2(self):
        P = self.P
        D = self.D
        P.phase_begin()
        kpe = P.palloc("kpe_sbuf", [128, S], BF16)
        self.DMA(kpe[0:64, :], D["kpe"].ap(), "kpe_sb", w=["kpe_sb"])
        self.DMA(kpe[64:128, :], D["kpe"].ap(), "kpe_sb2", w=["kpe_sb2"])
        Kn = [P.palloc("Kn%d" % s, [128, S], BF16) for s in range(2)]
        Vh = [P.palloc("Vh%d" % s, [128, 32, 128], BF16) for s in range(2)]
        NQ = 3
        qn_t = [P.palloc("qn_t%d" % s, [128, TT], BF16) for s in range(NQ)]
        qr_t = [P.palloc("qr_t%d" % s, [128, TT], BF16) for s in range(NQ)]
        NPT = 8
        pt = [P.palloc("pt%d" % s, [128, TT], BF16) for s in range(NPT)]
        accD = [P.palloc("accD%d" % s, [128, TT], F32) for s in range(2)]
        accP = [P.palloc("accP%d" % s, [128, TT], F32) for s in range(2)]
        lsb = [P.palloc("lsb%d" % s, [128, TT], F32) for s in range(2)]
        rinv = [P.palloc("rinv%d" % s, [128, TT], F32) for s in range(2)]
        ost = [P.palloc("ost%d" % s, [128, TT], BF16) for s in range(2)]
        scale = float((128 + 64) ** -0.5)
        cnt = dict(pt=0, q=0, ep=0, sb=0)

        def load_head(h):
            hs = h % 2
            self.DMA(Kn[hs][:], D["kn"].ap()[h], ("Kn", hs), w=[("Kn", hs)])
            self.DMA(Vh[hs][:], D["vS"].ap()[h].rearrange("(t p) d -> p t d", p=128), ("Vh", hs), w=[("Vh", hs)])

        def load_q(h, i):
            qs = cnt["q"] % NQ
            cnt["q"] += 1
            tsl = slice(i * TT, (i + 1) * TT)
            self.DMA(qn_t[qs][:], D["qn"].ap()[h][:, tsl], ("qn_t", qs), w=[("qn_t", qs)])
            self.DMA(qr_t[qs][0:64, :], D["qr"].ap()[h][:, tsl], ("qr_t", qs), w=[("qr_t", qs)])
            self.DMA(qr_t[qs][64:128, :], D["qr"].ap()[h][:, tsl], ("qr_t2", qs), w=[("qr_t2", qs)])
            return qs

        seq = [(h, i) for h in range(8) for i in range(NT)]
        load_head(0)
        qslots = {}
        qslots[seq[0]] = load_q(*seq[0])
        stream = []
        for n, (h, i) in enumerate(seq):
            nk = 4 * i + 4
            for kt in range(0, nk, 2):
                stream.append((n, h, i, kt, nk))
        state = {}

        def front(n, h, i, kt0, nk):
            if kt0 == 0:
                if i == 1 and h + 1 < 8:
                    load_head(h + 1)
                if n + 1 < len(seq):
                    qslots[seq[n + 1]] = load_q(*seq[n + 1])
                ep = cnt["ep"] % 2
                cnt["ep"] += 1
                state[n] = dict(ep=ep, pts={})
            hs = h % 2
            qs = qslots[(h, i)]
            banks = []
            c0s = [128 * max(0, kt0 + u - 4 * i) for u in range(2)]
            for u in range(2):
                bS = cnt["sb"] % 4
                cnt["sb"] += 1
                banks.append(bS)
                ksl = slice((kt0 + u) * 128, (kt0 + u + 1) * 128)
                self.MM(self.PS[bS][:, c0s[u]:], Kn[hs][:, ksl], qn_t[qs][:, c0s[u]:], True, False,
                        [("Kn", hs), ("qn_t", qs)], [("ps", bS)])
            for u in range(2):
                kt = kt0 + u
                bS = banks[u]
                ksl = slice(kt * 128, (kt + 1) * 128)
                rows = slice(64 * u, 64 * u + 64)
                diag = kt >= 4 * i
                self.MM(self.PS[bS][:, c0s[u]:], kpe[rows, ksl], qr_t[qs][rows, c0s[u]:], False, not diag,
                        ["kpe_sb", "kpe_sb2", ("qr_t", qs), ("qr_t2", qs)], [("ps", bS)])
            for u in range(2):
                kt = kt0 + u
                bS = banks[u]
                if kt >= 4 * i:
                    self.MM(self.PS[bS][:, c0s[u]:c0s[u] + 128], self.identb[:], self.cbig[:, 384:512], False, True,
                            ["identb", "cbig"], [("ps", bS)])
            for u in range(2):
                kt = kt0 + u
                bS = banks[u]
                ps_ = cnt["pt"] % NPT
                cnt["pt"] += 1
                self.ACT(pt[ps_][:, c0s[u]:], self.PS[bS][:, c0s[u]:], AF.Exp, [("ps", bS)], [("pt", ps_)], scale=scale)
                state[n]["pts"][kt] = ps_

        def back(n, h, i, kt0, nk):
            hs = h % 2
            ep = state[n]["ep"]
            bO = 4 + ep
            bL = 6 + ep
            tsl = slice(i * TT, (i + 1) * TT)
            for u in range(2):
                kt = kt0 + u
                ps_ = state[n]["pts"][kt]
                cc0 = 128 * max(0, kt - 4 * i)
                self.MM(self.PS[bO][:, cc0:], Vh[hs][:, kt, :], pt[ps_][:, cc0:], kt == 0, kt == nk - 1,
                        [("Vh", hs), ("pt", ps_)], [("ps", bO)])
                if kt == 0:
                    self.CP("dve", accD[ep][:], pt[ps_][:], [("pt", ps_)], [("accD", ep)])
                else:
                    c0 = 128 * max(0, kt - 4 * i)
                    self.TTo("dve", accD[ep][:, c0:], accD[ep][:, c0:], pt[ps_][:, c0:], ALU.add,
                             [("pt", ps_), ("accD", ep)], [("accD", ep)])
            if kt0 + 2 >= nk:
                def epilogue(ep=ep, bO=bO, bL=bL, h=h, tsl=tsl, i=i):
                    self.MM(self.PS[bL][:], self.ones_f[:], accD[ep][:], True, True, ["ones_f", ("accD", ep)], [("ps", bL)])
                    self.ACT(lsb[ep][:], self.PS[bL][:], AF.Ln, [("ps", bL)], [("lsb", ep)])
                    self.ACT(rinv[ep][:], lsb[ep][:], AF.Exp, [("lsb", ep)], [("rinv", ep)], scale=-1.0)
                    self.TTo("dve", ost[ep][:], self.PS[bO][:], rinv[ep][:], ALU.mult, [("ps", bO), ("rinv", ep)], [("ost", ep)])
                    self.DMA(D["oT"].ap()[h * 128:(h + 1) * 128, tsl], ost[ep][:], ("ost", ep), r=[("ost", ep)],
                             w=[("oT_d", h, i)])
                pending.append([2, epilogue])

        LA = 1
        pending = []
        for idx in range(len(stream) + LA):
            if idx < len(stream):
                front(*stream[idx])
            if idx - LA >= 0:
                back(*stream[idx - LA])
            for pe_ in list(pending):
                pe_[0] -= 1
                if pe_[0] < 0:
                    pe_[1]()
                    pending.remove(pe_)
        for pe_ in pending:
            pe_[1]()
        P.phase_end()

    def phase_C3(self):
        P = self.P
        D = self.D
        P.phase_begin()
        self._wst = 0
        Wo = P.palloc("Wo", [128, 8, 1024], BF16)
        stg = [P.palloc("wstg%d" % s, [128, 8, 256], F32) for s in range(4)]
        fn = P.palloc("fn", [128, 8], F32)
        self.DMA(fn[:], D["final_norm"].ap(), "fn", w=["fn"])
        WoR = [("Wo", c) for c in range(0, 1024, 256)]
        NS = self.alloc_norm_scratch()
        og = [P.palloc("og%d" % s, [128, 8, TT], BF16) for s in range(2)]
        gt = [P.palloc("gt%d" % s, [128, 8, TT], BF16) for s in range(2)]
        x1 = [P.palloc("x1_%d" % s, [128, 8, TT], F32) for s in range(2)]
        mm = P.palloc("mm", [128, 8, TT], BF16)
        x2 = P.palloc("x2", [128, 8, TT], F32)
        ot = P.palloc("ot", [128, 8, TT], F32)
        oTv = D["oT"].ap().rearrange("(c p) t -> p c t", p=128)
        gTv = D["gT1"].ap().rearrange("(c p) t -> p c t", p=128)
        x1v = D["x1T"].ap().rearrange("(c p) t -> p c t", p=128)
        outv = D["outT"].ap().rearrange("(c p) t -> p c t", p=128)

        def load(i):
            sl = i % 2
            tsl = slice(i * TT, (i + 1) * TT)
            self.DMA(og[sl][:], oTv[:, :, tsl], ("og", sl), w=[("og", sl)])
            self.DMA(gt[sl][:], gTv[:, :, tsl], ("gt", sl), w=[("gt", sl)])
            self.DMA(x1[sl][:], x1v[:, :, tsl], ("x1", sl), w=[("x1", sl)])

        load(0)
        self.load_w(Wo, D["c_w_out"].ap().rearrange("(c p) n -> p c n", p=128), 8, 1024, stg, "Wo")
        for i in range(NT):
            if i + 1 < NT:
                load(i + 1)
            sl = i % 2
            tsl = slice(i * TT, (i + 1) * TT)
            self.TTo("dve", mm[:], og[sl][:], gt[sl][:], ALU.mult, [("og", sl), ("gt", sl)], ["mm"])
            for dc in range(8):
                b = self.bank(0, 6)
                for kc in range(8):
                    self.MM(self.PS[b][:], Wo[:, kc, dc * 128:(dc + 1) * 128], mm[:, kc, :], kc == 0, kc == 7,
                            self.wcover("Wo", dc * 128, 128) + ["mm"], [("ps", b)])
                self.TTo("dve", x2[:, dc, :], self.PS[b][:], x1[sl][:, dc, :], ALU.add, [("ps", b), ("x1", sl)], [("x2", dc)])
            x2R = [("x2", dc) for dc in range(8)]
            rstd = self.rstd_tile(x2, 8, 1.0 / DM, NS, x2R)
            self.norm_apply(ot, x2, 8, fn, rstd, x2R, ("ot",), "fn")
            self.DMA(outv[:, :, tsl], ot[:], "ot", r=[("ot", c) for c in range(8)], w=[("out_d", i)])
        P.phase_end()

    def layer0(self):
        P = self.P
        D = self.D
        L = ExitStack()
        nc = self.nc

        def lal(name, shape, dt):
            return L.enter_context(nc.sbuf_tensor(name, list(shape), dt))

        R = self.R0 = {}
        R["KE"] = [lal("KE%d" % g, [128, S], BF16) for g in range(2)]
        R["KW"] = [lal("KW%d" % g, [64, S], BF16) for g in range(2)]
        R["Vs"] = lal("Vs", [128, 2, 32, 128], BF16)
        R["Vw"] = lal("Vw", [128, 2, 32, 128], BF16)
        R["KC"] = [lal("KC%d" % g, [64, 256], BF16) for g in range(2)]
        R["Vc"] = lal("Vc", [128, 2, 2, 128], BF16)
        R["SB"] = lal("SB", [128, S], BF16)
        R["tbig"] = lal("tbig_sb", [128, S], BF16)
        self.KCR = {"KCR": lal("KCR", [128, S], BF16), "VCR": lal("VCR", [128, S], BF16)}
        for nm in ("A1", "A1b", "A2", "A2b", "A3", "A4"):
            getattr(self, "phase_" + nm)()
            if self.stop_after == nm:
                break
        L.close()

    def load_seg(self, dst, dcol, src_w, scol, n, kcn, stg_ap_fn, tokname, chunk=512):
        engs = ["dve", "act"]
        for c0 in range(0, n, chunk):
            m = min(chunk, n - c0)
            e = engs[self._wst % 2]
            self._wst += 1
            stg, stok = stg_ap_fn()
            self.DMA(stg[:, 0:kcn, 0:m], src_w[:, :, scol + c0:scol + c0 + m], stok, w=[stok])
            self.CP(e, dst[:, 0:kcn, dcol + c0:dcol + c0 + m], stg[:, 0:kcn, 0:m], [stok], [(tokname, dcol + c0)])
            self.wreg_add(tokname, dcol + c0, dcol + c0 + m, (tokname, dcol + c0))

    def phase_A1(self):
        P = self.P
        D = self.D
        R = self.R0
        P.phase_begin()
        self._wst = 0
        NA = 2072
        Wa = P.palloc("Wa", [128, 8, NA], BF16)
        xt = P.palloc("xt", [128, 8, TT], F32)
        an = P.palloc("an", [128, 8], F32)
        self.DMA(an[:], D["a_norm"].ap(), "an", w=["an"])
        src = D["a_w_in"].ap().rearrange("(c p) n -> p c n", p=128)
        segs = [(0, 0, 512), (1024, 768, 128), (1280, 1024, 128), (1536, 512, 256), (1792, 896, 128),
                (1920, 1152, 128), (2048, 1280, 24)]
        nst = [0]

        def stgfn():
            k = nst[0] % 2
            nst[0] += 1
            return xt[:, :, 256 * k:256 * k + 256], ("xtH", k)

        for (dc, sc, n) in segs:
            self.load_seg(Wa, dc, src, sc, n, 8, stgfn, "Wa", chunk=256)
        WaR = [("Wa", dc) for (dc, sc, n) in segs]
        for h in range(8):
            self.make_rot(Wa, 8, 64 * h, 512 + 64 * h, self.wcover("Wa", 64 * h, 64), ("Wa", "rq", h))
            WaR += [("Wa", "rq", h, "a"), ("Wa", "rq", h, "b")]
        for g in range(2):
            self.make_rot(Wa, 8, 1024 + 64 * g, 1152 + 64 * g, self.wcover("Wa", 1024 + 64 * g, 64), ("Wa", "rs", g))
            self.make_rot(Wa, 8, 1280 + 64 * g, 1408 + 64 * g, self.wcover("Wa", 1280 + 64 * g, 64), ("Wa", "rw", g))
            WaR += [("Wa", "rs", g, "a"), ("Wa", "rs", g, "b"), ("Wa", "rw", g, "a"), ("Wa", "rw", g, "b")]
        NS = self.alloc_norm_scratch()
        RS = self.alloc_rope_scratch(128)
        xn2 = [P.palloc("xn%d" % s_, [128, 8, TT], BF16) for s_ in range(2)]
        qst = P.palloc("qst", [128, 4, TT], BF16)
        kstg = [P.palloc("kstg%d" % s_, [128, TT], BF16) for s_ in range(2)]
        gsb = P.palloc("gsb", [24, TT], F32)
        xv = D["xT"].ap().rearrange("(c p) t -> p c t", p=128)
        q0v = D["q0"].ap().rearrange("(pr two) p t -> (two p) pr t", two=2)
        evs = ["act", "dve"]
        nev = 0
        def pre(i):
            tsl = slice(i * TT, (i + 1) * TT)
            self.DMA(xt[:], xv[:, :, tsl], "xt", w=["xt", ("xtH", 0), ("xtH", 1)])
            rstd = self.rstd_tile(xt, 8, 1.0 / DM, NS, ["xt"])
            self.norm_apply(xn2[i % 2], xt, 8, an, rstd, ["xt"], ("xn", i % 2), "an")

        def body(i):
            nonlocal nev
            tsl = slice(i * TT, (i + 1) * TT)
            xn = xn2[i % 2]
            xnR = [("xn", i % 2, c) for c in range(8)]
            cs, sn, rtoks = ropeT[i]

            def proj8(ps_ap, col0, m, wtok):
                for kc in range(8):
                    self.MM(ps_ap, Wa[:, kc, col0:col0 + m], xn[:, kc, :], kc == 0, kc == 7, self.wcover("Wa", col0, m) + xnR, wtok)

            def roped(dst, c_main, c_rot, wtok):
                ba = self.bank(0, 7)
                bb = self.bank(0, 7)
                proj8(self.PS[ba][:, :], c_main, 128, [("ps", ba)])
                proj8(self.PS[bb][:, :], c_rot, 128, [("ps", bb)])
                self.rope_apply(dst, self.PS[ba][:, :], self.PS[bb][:, :], cs, sn, RS,
                                [("ps", ba), ("ps", bb)] + rtoks, wtok, npart=128)

            for pr in range(4):
                roped(qst[:, pr, :], 128 * pr, 512 + 128 * pr, [("qst", pr)])
            self.DMA(q0v[:, :, tsl], qst[:], "qst", r=[("qst", pr) for pr in range(4)], w=[("q0_d", i)])
            for kk_, (dsts, cm, cr) in enumerate(((R["KE"], 1024, 1152), (R["KW"], 1280, 1408))):
                roped(kstg[kk_][:], cm, cr, [("kstg", kk_)])
                self.CP("pool", dsts[0][0:64, tsl], kstg[kk_][0:64, :], [("kstg", kk_)], [("Kd", kk_, 0, i)])
                self.CP("act", dsts[1][0:64, tsl], kstg[kk_][64:128, :], [("kstg", kk_)], [("Kd", kk_, 1, i)])
            yield
            for (nm, col) in (("KCR", 1536), ("VCR", 1664)):
                b = self.bank(0, 7)
                proj8(self.PS[b][:], col, 128, [("ps", b)])
                self.CP(evs[nev % 2], self.KCR[nm][:, tsl], self.PS[b][:], [("ps", b)], [(nm, i)])
                nev += 1
                yield
            for st in range(4):
                b = self.bank(0, 7)
                for kc in range(8):
                    self.MM(self.PS[b][:, 0:256], xn[:, kc, st * 128:(st + 1) * 128], Wa[:, kc, 1792:2048], kc == 0, kc == 7,
                            self.wcover("Wa", 1792, 256) + xnR, [("ps", b)])
                kt = 4 * i + st
                for g in range(2):
                    self.CP("act", R["Vs"][:, g, kt, 0:64], self.PS[b][:, 64 * g:64 * g + 64], [("ps", b)], [("Vs", g, kt)])
                    self.CP("act", R["Vw"][:, g, kt, 0:64], self.PS[b][:, 128 + 64 * g:192 + 64 * g], [("ps", b)], [("Vw", g, kt)])
                yield
            b = self.bank(0, 7)
            proj8(self.PS[b][0:24, :], 2048, 24, [("ps", b)])
            self.ACT(gsb[:], self.PS[b][0:24, :], AF.Sigmoid, [("ps", b)], ["gsb"])
            self.DMA(D["gTa"].ap()[:, tsl], gsb[:], "gsb", r=["gsb"], w=[("gTa_d", i)])
        pre(0)
        ropeT = {0: self.rope_tables(0, RS)}
        for i in range(NT):
            gen = body(i)
            next(gen)
            aux = []
            if i + 1 < NT:
                P.defer_begin()
                pre(i + 1)
                ropeT[i + 1] = self.rope_tables(i + 1, RS)
                aux = P.defer_end()
            self.run_interleaved(gen, aux, 6)
        P.phase_end()

    def phase_A1b(self):
        P = self.P
        D = self.D
        P.phase_begin()
        self._wst = 0
        Wb = P.palloc("Wb", [128, 8, 2560], BF16)
        xt = [P.palloc("xt%d" % s, [128, 8, TT], F32) for s in range(1)]
        an = P.palloc("an", [128, 8], F32)
        cw = P.palloc("cw", [128, 4, 3], F32)
        self.DMA(an[:], D["a_norm"].ap(), "an", w=["an"])
        self.DMA(cw[:], D["a_conv_w"].ap(), "cw", w=["cw"])
        src = D["a_w_in"].ap().rearrange("(c p) n -> p c n", p=128)
        nst = [0]

        def stgfn():
            k = nst[0] % 2
            nst[0] += 1
            return xt[0][:, :, 256 * k:256 * k + 256], ("xtH", k)

        self.load_seg(Wb, 0, src, 1304, 2560, 8, stgfn, "Wb", chunk=256)
        WbR = [("Wb", c) for c in range(0, 2560, 512)]
        NS = self.alloc_norm_scratch()
        xn2 = [P.palloc("xn%d" % s_, [128, 8, TT], BF16) for s_ in range(2)]
        gast = [P.palloc("gast%d" % s_, [128, TT], BF16) for s_ in range(2)]
        cvst = P.palloc("cvst", [128, 4, TT], BF16)
        U = [P.palloc("U%d" % c, [128, TT + 2], F32) for c in range(4)]
        ccs = P.palloc("ccs", [128, TT], F32)
        yv = P.palloc("yv", [128, TT], F32)
        sg = P.palloc("sg", [128, TT], F32)
        for c in range(4):
            self.MEMSET("pool", U[c][:, 0:2], 0.0, [("Uh", c)])
        xv = D["xT"].ap().rearrange("(c p) t -> p c t", p=128)
        gav = D["gaT"].ap().rearrange("(c p) t -> p c t", p=128)
        mixv = D["mixT"].ap().rearrange("(c p) t -> p c t", p=128)

        def load_x(i):
            sl = 0
            self.DMA(xt[sl][:], xv[:, :, i * TT:(i + 1) * TT], ("xt", sl), w=[("xt", sl), ("xtH", 0), ("xtH", 1)])

        def pre(i):
            load_x(i)
            xtok = [("xt", 0)]
            rstd = self.rstd_tile(xt[0], 8, 1.0 / DM, NS, xtok)
            self.norm_apply(xn2[i % 2], xt[0], 8, an, rstd, xtok, ("xn", i % 2), "an")

        def body(i):
            sl = 0
            tsl = slice(i * TT, (i + 1) * TT)
            xn = xn2[i % 2]
            xnR = [("xn", i % 2, c) for c in range(8)]

            def proj8(ps_ap, col0, m, wtok):
                for kc in range(8):
                    self.MM(ps_ap, Wb[:, kc, col0:col0 + m], xn[:, kc, :], kc == 0, kc == 7, self.wcover("Wb", col0, m) + xnR, wtok)

            for c in range(4):
                b = self.bank(0, 7)
                proj8(self.PS[b][:], 128 * c, 128, [("ps", b)])
                self.ACT(gast[c % 2][:], self.PS[b][:], AF.Silu, [("ps", b)], [("gast", c % 2)])
                self.DMA(gav[:, c, tsl], gast[c % 2][:], ("gast", c % 2), r=[("gast", c % 2)], w=[("gaT_d", i, c)])
            for c in range(4):
                if c == 2:
                    yield
                bcb, bcc, bch, bgb = (self.bank(0, 7) for _ in range(4))
                proj8(self.PS[bcb][:], 512 + 128 * c, 128, [("ps", bcb)])
                if c >= 2:
                    yield
                proj8(self.PS[bcc][:], 1024 + 128 * c, 128, [("ps", bcc)])
                if c >= 2:
                    yield
                proj8(self.PS[bch][:], 1536 + 128 * c, 128, [("ps", bch)])
                if c >= 2:
                    yield
                proj8(self.PS[bgb][:], 2048 + 128 * c, 128, [("ps", bgb)])
                if c >= 2:
                    yield
                self.CP("act", ccs[:], self.PS[bcc][:], [("ps", bcc)], ["ccs"])
                self.TTo("dve", U[c][:, 2:TT + 2], self.PS[bch][:], ccs[:], ALU.mult, [("ps", bch), "ccs"], [("U", c)])
                ur = [("U", c), ("Uh", c), "cw"]
                self.TS("dve", yv[:], U[c][:, 0:TT], cw[:, c, 0:1], None, ALU.mult, None, ur, ["yv"])
                self.STT(yv[:], U[c][:, 1:TT + 1], cw[:, c, 1:2], yv[:], ALU.mult, ALU.add, ur + ["yv"], ["yv"])
                self.STT(yv[:], U[c][:, 2:TT + 2], cw[:, c, 2:3], yv[:], ALU.mult, ALU.add, ur + ["yv"], ["yv"])
                self.CP("pool", U[c][:, 0:2], U[c][:, TT:TT + 2], [("U", c)], [("Uh", c)])
                self.TTo("dve", ccs[:], self.PS[bcb][:], yv[:], ALU.mult, [("ps", bcb), "yv"], ["ccs"])
                self.ACT(sg[:], self.PS[bgb][:], AF.Silu, [("ps", bgb)], ["sg"])
                self.TTo("pool", cvst[:, c, :], ccs[:], sg[:], ALU.mult, ["ccs", "sg"], [("cvst", c)])
            self.DMA(mixv[:, 4:8, tsl], cvst[:], "cvst", r=[("cvst", c) for c in range(4)], w=[("mixc_d", i)])
        pre(0)
        for i in range(NT):
            gen = body(i)
            next(gen)
            aux = []
            if i + 1 < NT:
                P.defer_begin()
                pre(i + 1)
                aux = P.defer_end()
            self.run_interleaved(gen, aux, 8)
        P.phase_end()

    def phase_A2(self):
        P = self.P
        D = self.D
        R = self.R0
        P.phase_begin()
        stg = P.palloc("a2stg", [128, 32, 128], F32)
        W1 = {k: P.palloc("W1" + k, [128, 32, 128], BF16) for k in "kv"}
        W2k = P.palloc("W2k", [128, 128], BF16)
        W2v = P.palloc("W2v", [128, 64], BF16)
        peT = {k: P.palloc("peT" + k, [128, 32], BF16) for k in "kv"}
        s2 = P.palloc("a2s2", [128, 64], F32)
        s3 = P.palloc("a2s3", [128, 32], F32)
        for k, wn, pn in (("k", "a_w_ck1", "a_pe_kT"), ("v", "a_w_cv1", "a_pe_vT")):
            wv = D[wn].ap().rearrange("(l d) h -> d l h", d=64)
            self.DMA(stg[0:64], wv, "a2stg", w=["a2stg"])
            self.DMA(stg[64:128], wv, "a2stg", r=(), w=["a2stg_b"])
            self.CP("dve", W1[k][:], stg[:], ["a2stg", "a2stg_b"], ["W1" + k])
            self.DMA(s3[:], D[pn].ap(), "a2s3", w=["a2s3"])
            self.CP("dve", peT[k][:], s3[:], ["a2s3"], ["peT" + k])
        self.DMA(s2[:], D["a_w_ck2"].ap(), "a2s2", w=["a2s2"])
        self.CP("dve", W2k[:, 0:64], s2[:], ["a2s2"], ["W2k"])
        self.TS("dve", W2k[:, 64:96], s2[:, 32:64], -1.0, None, ALU.mult, None, ["a2s2"], ["W2kra"])
        self.CP("dve", W2k[:, 96:128], s2[:, 0:32], ["a2s2"], ["W2krb"])
        self.DMA(s2[:], D["a_w_cv2"].ap(), "a2s2", w=["a2s2"])
        self.CP("dve", W2v[:], s2[:], ["a2s2"], ["W2v"])
        cst = [P.palloc("cstg%d" % k_, [128, 2048], F32) for k_ in range(2)]
        ev = D["e30k"].ap()
        nc_ = 0
        for c0 in range(0, S, 2048):
            k_ = nc_ % 2
            nc_ += 1
            self.DMA(cst[k_][64:128, :], ev[:, c0:c0 + 2048], ("cstg", k_), w=[("cstg", k_)])
            for g in range(2):
                self.CP("dve" if g == 0 else "pool", R["KE"][g][64:128, c0:c0 + 2048], cst[k_][64:128, :], [("cstg", k_)],
                        [("KE_E", g, c0)])
        tv = D["tbig"].ap()
        for c0 in range(0, S, 2048):
            k_ = nc_ % 2
            nc_ += 1
            self.DMA(cst[k_][:, :], tv[:, c0:c0 + 2048], ("cstg", k_), w=[("cstg", k_)])
            self.TS("dve", R["tbig"][:, c0:c0 + 2048], cst[k_][:, :], 30000.0, -30000.0, ALU.mult, ALU.add, [("cstg", k_)],
                    [("tbig", c0)])
        self.MEMSET("pool", R["Vs"][:, :, :, 64:128], 1.0, ["Vs_ones"])
        self.MEMSET("pool", R["Vw"][:, :, :, 64:128], 1.0, ["Vw_ones"])
        self.MEMSET("pool", R["Vc"][:, :, :, 64:128], 1.0, ["Vc_ones"])
        RS = self.alloc_rope_scratch()
        posfull = P.palloc("posfull", [64, S], I32)
        self.DMA(posfull[:], bass.AP(D["pos"], 0, [[0, 64], [1, S]]), "posfull", w=["posfull"])
        cs, sn, rtoks = self.rope_tables(0, RS, src=posfull[:, 31:31 + 16 * 254 + 1:16], n=255, sb_src=True)
        hb = P.palloc("hb", [128, 1], F32)
        Hs = P.palloc("Hs", [128, 256], BF16)
        for g in range(2):
            self.MEMSET("pool", R["KC"][g][:], 0.0, [("KC", g)])
        for g in range(2):
            rows = slice(64 * g, 64 * g + 64)
            for k in "kv":
                raw = self.KCR["KCR" if k == "k" else "VCR"]
                bH = self.bank(0, 7)
                bB = self.bank(0, 7)
                for l in range(32):
                    self.MM(self.PS[bH][:, 0:255], W1[k][rows, l, :], raw[rows, l:l + 16 * 254 + 1:16], l == 0, l == 31,
                            ["W1" + k], [("ps", bH)])
                for l in range(32):
                    self.MM(self.PS[bB][:, 0:1], W1[k][rows, l, :], peT[k][rows, l:l + 1], l == 0, l == 31,
                            ["W1" + k, "peT" + k], [("ps", bB)])
                self.CP("act", hb[:], self.PS[bB][:, 0:1], [("ps", bB)], ["hb"])
                self.MEMSET("pool", Hs[:, 255:256], 0.0, ["Hs_pad"])
                self.ACT(Hs[:, 0:255], self.PS[bH][:, 0:255], AF.Silu, [("ps", bH), "hb"], ["Hs"], bias=hb[:, 0:1])
                if k == "k":
                    ba = self.bank(0, 7)
                    bb = self.bank(0, 7)
                    self.MM(self.PS[ba][0:64, 0:255], W2k[:, 0:64], Hs[:, 0:255], True, True, ["W2k", "Hs"], [("ps", ba)])
                    self.MM(self.PS[bb][0:64, 0:255], W2k[:, 64:128], Hs[:, 0:255], True, True,
                            ["W2kra", "W2krb", "Hs"], [("ps", bb)])
                    self.rope_apply(R["KC"][g][:, 0:255], self.PS[ba][0:64, 0:255], self.PS[bb][0:64, 0:255], cs, sn, RS,
                                    [("ps", ba), ("ps", bb)] + rtoks, [("KC", g)], n=255)
                else:
                    for ct in range(2):
                        bv = self.bank(0, 7)
                        self.MM(self.PS[bv][:, 0:64], Hs[:, ct * 128:(ct + 1) * 128], W2v[:], True, True,
                                ["Hs", "Hs_pad", "W2v"], [("ps", bv)])
                        self.CP("act", R["Vc"][:, g, ct, 0:64], self.PS[bv][:, 0:64], [("ps", bv)], [("Vc", g, ct)])
        P.phase_end()

    def phase_A2b(self):
        P = self.P
        D = self.D
        R = self.R0
        P.phase_begin()
        mbig = P.palloc("mbig_sb", [128, 512], F32)
        keep = P.palloc("keep_sb", [128, 128], F32)
        addt = P.palloc("addt_sb", [128, 128], F32)
        idf = P.palloc("idf", [128, 128], F32)
        idb = P.palloc("idb", [128, 128], BF16)
        self.DMA(mbig[:], D["mbig"].ap(), "mbig", w=["mbig"])
        self.DMA(keep[:], D["keep"].ap(), "keep", w=["keep"])
        self.DMA(addt[:], D["addt"].ap(), "addt", w=["addt"])
        self.DMA(idf[:], D["ident"].ap(), "idf", w=["idf"])
        self.CP("dve", idb[:], idf[:], ["idf"], ["idb"])
        qt = [P.palloc("qt%d" % s, [64, 8, TT], BF16) for s in range(2)]
        sm = [P.palloc("sm%d" % s, [128, 256], F32) for s in range(2)]
        ex = [P.palloc("ex%d" % s, [128, 256], F32) for s in range(2)]
        mx = [P.palloc("mx%d" % s, [128, 1], F32) for s in range(2)]
        rs = [P.palloc("rs%d" % s, [128, 1], F32) for s in range(2)]
        ri = [P.palloc("ri%d" % s, [128, 1], F32) for s in range(2)]
        acc = P.palloc("acc", [128, 264], F32)
        imp = P.palloc("imp", [128, 64], F32)
        imp2 = P.palloc("imp2", [128, 64], F32)
        imp3 = P.palloc("imp3", [128, 64], F32)
        m8a = P.palloc("m8a", [128, 8], F32)
        m8b = P.palloc("m8b", [128, 8], F32)
        selp = [P.palloc("selp%d" % g, [128, 128], BF16) for g in range(2)]
        for g in range(2):
            self.MEMSET("pool", selp[g][:], 0.0, [("selp", g)])
        q0v = D["q0"].ap().rearrange("h p t -> p h t")
        scale = 0.125
        nh = 0

        def load_q(i):
            sl = i % 2
            self.DMA(qt[sl][:], q0v[:, :, i * TT:(i + 1) * TT], ("qt", sl), w=[("qt", sl)])

        mbb = P.palloc("mbb", [128, 512], BF16)
        self.CP("dve", mbb[:], mbig[:], ["mbig"], ["mbb"])
        rs4 = P.palloc("rs4", [128, 4], F32)
        ri4 = P.palloc("ri4", [128, 4], F32)
        ex4 = [P.palloc("ex4_%d" % s_, [128, 256], F32) for s_ in range(4)]
        G2 = []
        for g in range(4):
            d = dict(
                acc=P.palloc("acc_g%d" % g, [128, 264], F32), imp=P.palloc("imp_g%d" % g, [128, 64], F32),
                imp2=P.palloc("imp2_g%d" % g, [128, 64], F32), imp3=P.palloc("imp3_g%d" % g, [128, 64], F32),
                m8a=P.palloc("m8a_g%d" % g, [128, 8], F32), m8b=P.palloc("m8b_g%d" % g, [128, 8], F32),
                rs4=P.palloc("rs4_g%d" % g, [128, 4], F32), ri4=P.palloc("ri4_g%d" % g, [128, 4], F32),
                ex4=[P.palloc("ex4_g%d_%d" % (g, k), [128, 256], F32) for k in range(4)],
                selp=P.palloc("selp_c%d" % g, [128, 128], BF16))
            self.MEMSET("pool", d["acc"][:], 0.0, [("acc", g)])
            self.MEMSET("pool", d["selp"][:], 0.0, [("selpc", g)])
            G2.append(d)

        def chain(i, g, st, sl):
            c = 2 * (st % 2) + g
            d = G2[c]
            selp_ = d["selp"]
            acc_, imp_, imp2_, imp3_, m8a_, m8b_, rs4_, ri4_, ex4_ = (d[k] for k in
                                                                   ("acc", "imp", "imp2", "imp3", "m8a", "m8b", "rs4", "ri4", "ex4"))
            T = lambda nm: (nm, c)
            sidx = 4 * i + st
            t0 = 128 * sidx
            for hp in range(2):
                b = c
                for hh2 in range(2):
                    h = 4 * g + 2 * hp + hh2
                    self.MM(self.PS[b][:, hh2 * 256:(hh2 + 1) * 256], qt[sl][:, h, st * 128:(st + 1) * 128],
                            R["KC"][g][:, :], True, False, [("qt", sl), ("KC", g)], [("ps", b)])
                    self.MM(self.PS[b][:, hh2 * 256:(hh2 + 1) * 256], self.identb[:],
                            mbb[:, 248 - 8 * sidx:504 - 8 * sidx], False, True, ["identb", "mbb"], [("ps", b)])
                for hh2 in range(2):
                    k = 2 * hp + hh2
                    self.ACT(ex4_[k][:], self.PS[b][:, hh2 * 256:(hh2 + 1) * 256], AF.Exp, [("ps", b)],
                             [("ex4", c, k), ("rs4", c, k)], scale=scale, accum_out=rs4_[:, k:k + 1])
            yield
            rtk = [("rs4", c, k) for k in range(4)]
            self.TS("dve", ri4_[:], rs4_[:], 1e-30, None, ALU.max, None, rtk, [T("ri4")])
            yield
            self.P.dve(lambda h_: h_.reciprocal(out=ri4_[:], in_=ri4_[:]), [T("ri4")], [T("ri4")])
            yield
            self.TS("dve", acc_[:, 1:257], ex4_[0][:], ri4_[:, 0:1], None, ALU.mult, None, [("ex4", c, 0), T("ri4"), T("acc")], [T("acc")])
            yield
            for k in range(1, 4):
                self.STT(acc_[:, 1:257], ex4_[k][:], ri4_[:, k:k + 1], acc_[:, 1:257], ALU.mult, ALU.add,
                         [("ex4", c, k), T("ri4"), T("acc")], [T("acc")])
                yield
            self.TTo("dve", imp_[:], acc_[:, 0:253:4], acc_[:, 1:254:4], ALU.add, [T("acc")], [T("imp")])
            yield
            for j in (2, 3, 4):
                self.TTo("dve", imp_[:], imp_[:], acc_[:, j:j + 253:4], ALU.add, [T("acc"), T("imp")], [T("imp")])
                yield
            self.TTo("dve", imp2_[:], imp_[:], keep[:, 64 - 2 * sidx:128 - 2 * sidx], ALU.mult, [T("imp"), "keep"], [T("imp2")])
            yield
            self.TTo("dve", imp2_[:], imp2_[:], addt[:, 64 - 2 * sidx:128 - 2 * sidx], ALU.add, [T("imp2"), "addt"], [T("imp2")])
            yield
            self.MEMSET("dve", imp2_[:, 0:1], 1.0e4, [T("imp2")])
            yield
            self.P.dve(lambda h_: h_.max(out=m8a_[:], in_=imp2_[:]), [T("imp2")], [T("m8a")])
            yield
            self.P.dve(lambda h_: h_.match_replace(out=imp3_[:], in_to_replace=m8a_[:], in_values=imp2_[:], imm_value=-1.0e9),
                       [T("imp2"), T("m8a")], [T("imp3")])
            yield
            self.P.dve(lambda h_: h_.max(out=m8b_[:], in_=imp3_[:]), [T("imp3")], [T("m8b")])
            yield
            self.TS("dve", selp_[:, 64 * g:64 * g + 64], imp2_[:], m8b_[:, 7:8], 1.0, ALU.is_ge, ALU.subtract,
                    [T("imp2"), T("m8b"), ("selpc", c)], [("selpc", c)])
            bt = 4 + c
            pst = self.PS[bt].bitcast(BF16)
            self.P.pe(lambda h_, o=pst[:, 0:128], a=selp_[:]: h_.transpose(o, a, self.identb[:]),
                      [("selpc", c), "identb"], [("ps", bt)])
            self.CP("act", R["SB"][64 * g:64 * g + 64, t0:t0 + 128], pst[64 * g:64 * g + 64, 0:128],
                    [("ps", bt)], [("SB", g, sidx)])

        cnt2 = [0, 0]
        load_q(0)
        for i in range(NT):
            if i + 1 < NT:
                load_q(i + 1)
            sl = i % 2
            for st0 in range(0, 4, 2):
                gens = [chain(i, g, st0 + u, sl) for u in range(2) for g in range(2)]
                alive = [True] * 4
                while any(alive):
                    for gi in range(4):
                        if alive[gi]:
                            try:
                                next(gens[gi])
                            except StopIteration:
                                alive[gi] = False
        P.phase_end()

    def phase_A3(self):
        P = self.P
        D = self.D
        R = self.R0
        P.phase_begin()
        NQ = 3
        Qs = [P.palloc("Qs%d" % s, [128, TT], BF16) for s in range(NQ)]
        Gbc = [P.palloc("Gbc%d" % s, [64, 3, TT], F32) for s in range(3)]
        gat = [P.palloc("gat%d" % s, [64, TT], BF16) for s in range(3)]
        NPT = 6
        pt = [P.palloc("pt%d" % s, [128, TT], BF16) for s in range(NPT)]
        lsb = [P.palloc("lsb%d" % s, [64, TT], F32) for s in range(3)]
        wb = [P.palloc("wb%d" % s, [64, TT], F32) for s in range(3)]
        ob = [P.palloc("ob%d" % s, [64, TT], F32) for s in range(3)]
        mst = [P.palloc("mst%d" % s, [64, TT], BF16) for s in range(2)]
        scale = 0.125
        cnt = dict(pt=0, q=0, sb=0, ob=0, mk=0, ev=0)
        gTa_t = D["gTa"]

        def load_q(h, i):
            qs = cnt["q"] % NQ
            cnt["q"] += 1
            g = h // 4
            tsl = slice(i * TT, (i + 1) * TT)
            self.DMA(Qs[qs][0:64, :], D["q0"].ap()[h][:, tsl], ("Qs", qs), w=[("Qs", qs)])
            self.DMA(Qs[qs][64:128, :], R["SB"][64 * g:64 * g + 64, tsl], ("QsB", qs), w=[("QsB", qs)])
            e2 = (cnt["q"] - 1) % 3
            src = bass.AP(gTa_t, 3 * h * S + i * TT, [[0, 64], [S, 3], [1, TT]])
            self.DMA(Gbc[e2][:], src, ("Gbc", e2), w=[("Gbc", e2)])
            self.DMA(gat[e2][:], D["gaT"].ap()[64 * h:64 * h + 64, tsl], ("gat", e2), w=[("gat", e2)])
            return qs, e2

        eps18 = P.palloc("eps18", [128, 1], F32)
        self.MEMSET("dve", eps18[:], 1e-18, ["eps18"])
        seq = [(h, i) for h in range(8) for i in range(NT)]
        slots = {seq[0]: load_q(*seq[0])}
        stream = []
        for n, (h, i) in enumerate(seq):
            g = h // 4
            tl = []
            cts = [0] + ([1] if i >= 4 else [])
            for n_, ct in enumerate(cts):
                off = 512 * i - 2048 * ct
                m_ap = None if off >= 2063 else R["tbig"][:, off:off + TT]
                tl.append(dict(br=0, lhsT=R["KC"][g][:, ct * 128:(ct + 1) * 128], full=False, ktok=[("KC", g)], mask=m_ap,
                               cols=(0, TT), mcols=(0, TT),
                               V=R["Vc"][:, g, ct, :], vtok=[("Vc", g, ct), "Vc_ones"], first=n_ == 0, last=n_ == len(cts) - 1))
            nk = 4 * i + 4
            for kt in range(nk):
                ksl = slice(kt * 128, (kt + 1) * 128)
                m_ap = None
                cols = (0, TT)
                mcols = None
                if kt >= 4 * i:
                    j = kt - 4 * i
                    m_ap = self.cbig[:, 384:512]
                    cols = (128 * j, TT)
                    mcols = (128 * j, 128 * j + 128)
                tl.append(dict(br=1, lhsT=R["KE"][g][:, ksl], full=True, ktok=[("KE", g)], mask=m_ap, cols=cols, mcols=mcols,
                               V=R["Vs"][:, g, kt, :], vtok=[("Vs", g)], first=kt == 0, last=kt == nk - 1))
            k0 = max(0, 4 * i - 4)
            wkts = [4 * i] + [kt for kt in range(k0, nk) if kt != 4 * i]
            for wn, kt in enumerate(wkts):
                ksl = slice(kt * 128, (kt + 1) * 128)
                if kt >= 4 * i:
                    j = kt - 4 * i
                    m_ap = self.cbig[:, 384:512]
                    cols = (128 * j, TT)
                else:
                    j = kt - (4 * i - 4)
                    m_ap = self.wbig[:, 384:512]
                    cols = (0, 128 * j + 128)
                mcols = (128 * j, 128 * j + 128)
                tl.append(dict(br=2, lhsT=R["KW"][g][:, ksl], full=False, ktok=[("KW", g)], mask=m_ap, cols=cols, mcols=mcols,
                               V=R["Vw"][:, g, kt, :], vtok=[("Vw", g)], first=wn == 0, last=wn == len(wkts) - 1))
            for ti, t in enumerate(tl):
                t.update(n=n, h=h, i=i, gfirst=ti == 0, glast=ti == len(tl) - 1)
                stream.append(t)
        state = {}

        def front(t):
            n, h, i = t["n"], t["h"], t["i"]
            if t["gfirst"]:
                if n + 1 < len(seq):
                    slots[seq[n + 1]] = load_q(*seq[n + 1])
                bOs = []
                for _ in range(3):
                    bOs.append(3 + cnt["ob"] % 5)
                    cnt["ob"] += 1
                state[n] = dict(bOs=bOs)
            qs, e2 = slots[(h, i)]
            bS = cnt["sb"] % 3
            cnt["sb"] += 1
            c0, c1 = t["cols"]
            rhs = Qs[qs][:, c0:c1] if t["full"] else Qs[qs][0:64, c0:c1]
            msk = t["mask"] is not None
            self.MM(self.PS[bS][:, c0:c1], t["lhsT"], rhs, True, not msk, t["ktok"] + [("Qs", qs), ("QsB", qs)], [("ps", bS)])
            if msk:
                m0, m1 = t["mcols"]
                self.MM(self.PS[bS][:, m0:m1], self.identb[:], t["mask"], False, True, ["identb", "cbig", "wbig"], [("ps", bS)])
            p_ = cnt["pt"] % NPT
            cnt["pt"] += 1
            self.ACT(pt[p_][:, c0:c1], self.PS[bS][:, c0:c1], AF.Exp, [("ps", bS)], [("pt", p_)], scale=scale)
            t["p_"] = p_

        def back(t):
            n, h, i = t["n"], t["h"], t["i"]
            p_ = t["p_"]
            bO = state[n]["bOs"][t["br"]]
            c0, c1 = t["cols"]
            self.MM(self.PS[bO][:, c0:c1], t["V"], pt[p_][:, c0:c1], t["first"], t["last"], t["vtok"] + [("pt", p_)], [("ps", bO)])
            if not t["glast"]:
                return
            P.defer_begin()
            qs, e2 = slots[(h, i)]
            tsl = slice(i * TT, (i + 1) * TT)
            for bi, bO in enumerate(state[n]["bOs"]):
                if bi == 0:
                    self.ACT(lsb[bi][:], self.PS[bO][64:128, :], AF.Ln, [("ps", bO), "eps18"], [("lsb", bi)], bias=eps18[0:64, 0:1])
                else:
                    self.ACT(lsb[bi][:], self.PS[bO][64:128, :], AF.Ln, [("ps", bO)], [("lsb", bi)])
                self.ACT(wb[bi][:], lsb[bi][:], AF.Exp, [("lsb", bi)], [("wb", bi)], scale=-1.0)
                self.TTo("pool", wb[bi][:], wb[bi][:], Gbc[e2][:, bi, :], ALU.mult, [("wb", bi), ("Gbc", e2)], [("wb", bi)])
                self.TTo("dve", ob[bi][:], self.PS[bO][0:64, :], wb[bi][:], ALU.mult, [("ps", bO), ("wb", bi)], [("ob", bi)])
            self.TTo("pool", ob[0][:], ob[0][:], ob[1][:], ALU.add, [("ob", 0), ("ob", 1)], [("ob", 0)])
            self.TTo("pool", ob[0][:], ob[0][:], ob[2][:], ALU.add, [("ob", 0), ("ob", 2)], [("ob", 0)])
            ms = n % 2
            self.TTo("pool", mst[ms][:], ob[0][:], gat[e2][:], ALU.mult, [("ob", 0), ("gat", e2)], [("mst", ms)])
            self.DMA(D["mixT"].ap()[64 * h:64 * h + 64, tsl], mst[ms][:], ("mst", ms), r=[("mst", ms)], w=[("mix_d", h, i)])
            epq.extend(P.defer_end())

        LA = 2
        epq = []
        for idx in range(len(stream) + LA):
            if idx < len(stream):
                front(stream[idx])
            if idx - LA >= 0:
                back(stream[idx - LA])
            if epq:
                P.replay(epq[:2])
                del epq[:2]
        P.replay(epq)
        P.phase_end()

    def phase_A4(self):
        P = self.P
        D = self.D
        P.phase_begin()
        self._wst = 0
        Wo = P.palloc("Woa", [128, 8, 1024], BF16)
        xt = [P.palloc("xt%d" % s, [128, 8, TT], F32) for s in range(2)]
        mx = [P.palloc("mixs%d" % s, [128, 8, TT], BF16) for s in range(2)]
        x1 = P.palloc("x1o", [128, 8, TT], F32)
        nst = [0]

        wstg = [P.palloc("a4stg%d" % s_, [128, 8, 256], F32) for s_ in range(2)]

        def stgfn():
            sl = nst[0] % 2
            nst[0] += 1
            return wstg[sl], ("a4stg", sl)

        WoR = [("Woa", 0), ("Woa", 512)]
        xv = D["xT"].ap().rearrange("(c p) t -> p c t", p=128)
        mixv = D["mixT"].ap().rearrange("(c p) t -> p c t", p=128)
        x1v = D["x1T"].ap().rearrange("(c p) t -> p c t", p=128)

        def load(i):
            sl = i % 2
            tsl = slice(i * TT, (i + 1) * TT)
            self.DMA(xt[sl][:], xv[:, :, tsl], ("xt", sl), w=[("xt", sl)])
            self.DMA(mx[sl][:], mixv[:, :, tsl], ("mixs", sl), w=[("mixs", sl)])

        load(0)
        for c0_ in range(0, 1024, 256):
            self.load_seg(Wo, c0_, D["a_w_out"].ap().rearrange("(c p) n -> p c n", p=128), c0_, 256, 8, stgfn, "Woa")
        for i in range(NT):
            if i + 1 < NT:
                load(i + 1)
            sl = i % 2
            tsl = slice(i * TT, (i + 1) * TT)
            for dc in range(8):
                b = self.bank(0, 8)
                for kc in range(8):
                    self.MM(self.PS[b][:], Wo[:, kc, dc * 128:(dc + 1) * 128], mx[sl][:, kc, :], kc == 0, kc == 7,
                            self.wcover("Woa", dc * 128, 128) + [("mixs", sl)], [("ps", b)])
                self.TTo("dve", x1[:, dc, :], self.PS[b][:], xt[sl][:, dc, :], ALU.add, [("ps", b), ("xt", sl)], [("x1o", dc)])
            self.DMA(x1v[:, :, tsl], x1[:], "x1o", r=[("x1o", dc) for dc in range(8)], w=[("x1_d", i)])
        P.phase_end()

    def declare_io(self):
        mode = self.mode
        self.din("pos", [1, S], I32)
        self.din("invf", [128, 1])
        self.din("cbig", [128, 896])
        self.din("ident", [128, 128])
        if mode in ("full", "L0"):
            self.din("xT", [DM, S])
            self.din("a_norm", [128, 8])
            self.din("a_w_in", [DM, 3864])
            self.din("a_pe_kT", [128, 32])
            self.din("a_pe_vT", [128, 32])
            self.din("a_w_ck1", [2048, 128])
            self.din("a_w_ck2", [128, 64])
            self.din("a_w_cv1", [2048, 128])
            self.din("a_w_cv2", [128, 64])
            self.din("a_conv_w", [128, 4, 3])
            self.din("a_w_out", [DM, DM])
            self.din("e30k", [64, S])
            self.din("tbig", [128, S])
            self.din("mbig", [128, 512])
            self.din("keep", [128, 128])
            self.din("addt", [128, 128])
            for nm, shp, dt in (("q0", [8, 64, S], BF16), ("gTa", [24, S], F32), ("gaT", [512, S], BF16),
                                ("mixT", [DM, S], BF16)):
                self.dscr(nm, shp, dt)
        if mode in ("full", "L1"):
            self.din("c_norm", [128, 8])
            self.din("c_w_in", [DM, 1600])
            self.din("c_q_norm", [128, 2])
            self.din("c_kv_norm", [128, 2])
            self.din("c_w_uq", [256, 1536])
            self.din("c_w_ukv", [256, 2048])
            self.din("c_w_out", [DM, DM])
            self.din("final_norm", [128, 8])
            self.dout("outT", [DM, S])
            for nm, shp, dt in (("gT1", [DM, S], BF16), ("qn", [8, 128, S], BF16), ("qr", [8, 64, S], BF16),
                                ("kn", [8, 128, S], BF16), ("kpe", [64, S], BF16), ("vS", [8, S, 128], BF16),
                                ("oT", [DM, S], BF16)):
                self.dscr(nm, shp, dt)
        if mode == "L1":
            self.din("x1T", [DM, S])
        elif mode == "full":
            self.dscr("x1T", [DM, S], F32)
        else:
            self.dout("x1T", [DM, S])

    def build(self):
        self.declare_io()
        self.setup_globals()
        if self.mode in ("full", "L0"):
            self.layer0()
        if self.mode in ("full", "L1"):
            for nm in ("C1", "C2", "C3"):
                getattr(self, "phase_" + nm)()
                if self.stop_after == nm:
                    break
        self.P.barrier()
        self.P.emit()
        self.P.close()
        return self.nc


def _vec(v, k):
    return np.ascontiguousarray(np.asarray(v, np.float32).reshape(k, 128).T)


def const_inputs():
    half = 32
    invf32 = (np.float32(10000.0) ** (-(np.arange(half, dtype=np.float32) / np.float32(half)))).astype(np.float32)
    invf = np.concatenate([invf32, invf32, invf32, invf32])[:, None].astype(np.float32)
    kk = np.arange(128)[:, None]
    w = np.arange(896)[None, :]
    cbig = (kk <= (w - 384)).astype(np.float32)
    return {"invf": invf, "cbig": cbig, "ident": np.eye(128, dtype=np.float32)}


def layer1_inputs(c_norm, c_w_in, c_q_norm, c_kv_norm, c_w_uq, c_w_ukv, c_w_out, final_norm):
    return {
        "c_norm": _vec(c_norm[0], 8), "c_w_in": np.ascontiguousarray(c_w_in[0], np.float32),
        "c_q_norm": _vec(c_q_norm[0], 2), "c_kv_norm": _vec(c_kv_norm[0], 2),
        "c_w_uq": np.ascontiguousarray(c_w_uq[0], np.float32), "c_w_ukv": np.ascontiguousarray(c_w_ukv[0], np.float32),
        "c_w_out": np.ascontiguousarray(c_w_out[0], np.float32), "final_norm": _vec(final_norm, 8),
    }


def const_inputs0():
    c = {}
    n = np.arange(64)[:, None]
    key = np.arange(S)[None, :]
    c["e30k"] = ((key // 64) == n).astype(np.float32) * np.float32(30000.0)
    cc = np.arange(128)[:, None]
    w = np.arange(S)[None, :]
    c["tbig"] = ((16 * cc + 31) <= w).astype(np.float32)
    tl = np.arange(128)[:, None]
    u = np.arange(512)[None, :]
    c["mbig"] = np.where((16 * (u - 248) + 31) <= tl, 0.0, NEG).astype(np.float32)
    u = np.arange(128)[None, :]
    q = tl // 64
    c["keep"] = ((u - 64) <= (q - 2)).astype(np.float32)
    addt = np.zeros((128, 128), np.float32)
    addt[((u - 64) == q) | ((u - 64) == (q - 1))] = 1.0e4
    addt[(u - 64) > q] = -1.0
    c["addt"] = addt
    return c


def layer0_inputs(a_norm, a_w_in, a_pe_k, a_pe_v, a_w_ck1, a_w_ck2, a_w_cv1, a_w_cv2, a_conv_w, a_w_out):
    f = lambda a: np.ascontiguousarray(a, np.float32)
    pekT = f(a_pe_k[0]).T
    pevT = f(a_pe_v[0]).T
    return {
        "a_norm": _vec(a_norm[0], 8), "a_w_in": f(a_w_in[0]),
        "a_pe_kT": f(np.concatenate([pekT, pekT], 0)), "a_pe_vT": f(np.concatenate([pevT, pevT], 0)),
        "a_w_ck1": f(a_w_ck1[0]), "a_w_ck2": f(a_w_ck2[0]), "a_w_cv1": f(a_w_cv1[0]), "a_w_cv2": f(a_w_cv2[0]),
        "a_conv_w": f(f(a_conv_w[0]).T.reshape(4, 128, 3).transpose(1, 0, 2)), "a_w_out": f(a_w_out[0]),
    }


_NC_CACHE = {}


def _get_nc():
    if "nc" not in _NC_CACHE:
        _NC_CACHE["nc"] = Builder("full").build()
    return _NC_CACHE["nc"]


def kernel(x, positions, a_norm, a_w_in, a_pe_k, a_pe_v, a_w_ck1, a_w_ck2, a_w_cv1, a_w_cv2,
           a_conv_w, a_w_out, c_norm, c_w_in, c_q_norm, c_kv_norm, c_w_uq, c_w_ukv, c_w_out,
           final_norm):
    x = np.asarray(x, np.float32)
    positions = np.asarray(positions).astype(np.int32)
    nb = x.shape[0]
    shared = dict(const_inputs())
    shared.update(const_inputs0())
    shared.update(layer0_inputs(np.asarray(a_norm), np.asarray(a_w_in), np.asarray(a_pe_k), np.asarray(a_pe_v),
                                np.asarray(a_w_ck1), np.asarray(a_w_ck2), np.asarray(a_w_cv1), np.asarray(a_w_cv2),
                                np.asarray(a_conv_w), np.asarray(a_w_out)))
    shared.update(layer1_inputs(np.asarray(c_norm), np.asarray(c_w_in), np.asarray(c_q_norm), np.asarray(c_kv_norm),
                                np.asarray(c_w_uq), np.asarray(c_w_ukv), np.asarray(c_w_out), np.asarray(final_norm)))
    in_maps = []
    for b in range(nb):
        m = dict(shared)
        m["xT"] = np.ascontiguousarray(x[b].T)
        m["pos"] = np.ascontiguousarray(positions[b:b + 1])
        in_maps.append(m)
    nc = _get_nc()
    res = run_bass_kernel_spmd(nc, in_maps, core_ids=list(range(nb)))
    out = np.stack([np.ascontiguousarray(r["outT"].T) for r in res.results], axis=0)
    return out.astype(np.float32)
```

```python
import math
from contextlib import ExitStack

import numpy as np
import concourse.bass as bass
import concourse.mybir as mybir
from concourse.bass_utils import run_bass_kernel_spmd

F32 = mybir.dt.float32
BF16 = mybir.dt.bfloat16
I32 = mybir.dt.int32
F32R = mybir.dt.float32r
ALU = mybir.AluOpType
AF = mybir.ActivationFunctionType

ENGS = ("pe", "act", "dve", "pool", "sp")

S = 4096
DM = 1024
TT = 512
NT = S // TT
NEG = -30000.0


class Prog:
    def __init__(self, nc, strict=("act", "dve", "pool")):
        self.nc = nc
        self.ops = []
        self.last_w = {}
        self.readers = {}
        self.strict = set(strict)
        self.es = ExitStack()
        self.pes = None
        self.dma_keys = []
        self.last_real = {}
        self.dma_since = []
        self.bank_rr = 0
        self.last_dma_by_key = {}
        self.deferred = None
        self.max_ops = None
        self.n_real = 0
        self.last_desc = None

    def sbuf(self, name, shape, dt):
        return self.es.enter_context(self.nc.sbuf_tensor(name, list(shape), dt))

    def psum(self, name, shape, dt=F32):
        return self.es.enter_context(self.nc.psum_tensor(name, list(shape), dt))

    def phase_begin(self):
        self.pes = ExitStack()
        self.phase_no = getattr(self, "phase_no", 0) + 1

    def palloc(self, name, shape, dt):
        return self.pes.enter_context(self.nc.sbuf_tensor("%s_p%d" % (name, self.phase_no), list(shape), dt))

    def phase_end(self):
        self.barrier()
        self.pes.close()
        self.pes = None

    def defer_begin(self):
        self.deferred = []

    def defer_end(self):
        d = self.deferred
        self.deferred = None
        return d

    def replay(self, lst):
        for a in lst:
            self.add(*a)

    def add(self, eng, fn, reads=(), writes=(), dma=None, ndma=1, xdeps=None, real=True):
        if self.deferred is not None:
            self.deferred.append((eng, fn, tuple(reads), tuple(writes), dma, ndma, xdeps, real))
            return None
        if real and self.max_ops is not None:
            if self.n_real >= self.max_ops:
                return None
            self.n_real += 1
            self.last_desc = (eng, list(writes), dma)
        i = len(self.ops)
        deps = set()
        if xdeps is not None:
            deps.update(xdeps)
        for t in reads:
            if t in self.last_w:
                deps.add(self.last_w[t])
        for t in writes:
            if t in self.last_w:
                deps.add(self.last_w[t])
            for r in self.readers.get(t, {}).values():
                if isinstance(r, list):
                    deps.update(r)
                else:
                    deps.add(r)
        op = dict(i=i, eng=eng, fn=fn, dma=dma, ndma=ndma, deps=[], signal=False)
        deps = {(self.last_dma_by_key[self.ops[d]["dma"]] if self.ops[d]["dma"] is not None else d) for d in deps}
        for d in sorted(deps):
            A = self.ops[d]
            if A["dma"] is None and A["eng"] == eng and eng not in self.strict:
                continue
            op["deps"].append(d)
            A["signal"] = True
        for t in reads:
            rd = self.readers.setdefault(t, {})
            if dma is None:
                rd[eng] = i
            else:
                rd.setdefault("dma", []).append(i)
        for t in writes:
            self.last_w[t] = i
            self.readers[t] = {}
        if dma is not None:
            if dma not in self.dma_keys:
                self.dma_keys.append(dma)
            self.dma_since.append(i)
            self.last_dma_by_key[dma] = i
        elif real:
            self.last_real[eng] = i
        self.ops.append(op)
        return i

    def barrier(self):
        lr = dict(self.last_real)
        dmas = list(self.dma_since)
        for e in ENGS:
            x = [v for k, v in lr.items() if k != e] + dmas
            self.add(e, lambda h: None, xdeps=x, real=False)
        self.dma_since = []
        self.last_w = {}
        self.readers = {}

    def pe(self, fn, r=(), w=()):
        return self.add("pe", fn, r, w)

    def act(self, fn, r=(), w=()):
        return self.add("act", fn, r, w)

    def dve(self, fn, r=(), w=()):
        return self.add("dve", fn, r, w)

    def pool(self, fn, r=(), w=()):
        return self.add("pool", fn, r, w)

    def dma(self, fn, key, r=(), w=(), ndma=1, eng="sp"):
        return self.add(eng, fn, r, w, dma=key, ndma=ndma)

    def emit(self):
        nc = self.nc
        es = self.es
        sems = {}
        for e in ENGS:
            sems[("eng", e)] = es.enter_context(nc.semaphore("s_" + e))
        for n, k in enumerate(self.dma_keys):
            sems[("dma", k)] = es.enter_context(nc.semaphore("d%d" % n))
        eng_count = {e: 0 for e in ENGS}
        dma_count = {}
        for op in self.ops:
            if op["dma"] is None:
                if op["signal"]:
                    eng_count[op["eng"]] += 1
                    op["sem"] = ("eng", op["eng"])
                    op["val"] = eng_count[op["eng"]]
            else:
                k = op["dma"]
                dma_count[k] = dma_count.get(k, 0) + 16 * op["ndma"]
                op["sem"] = ("dma", k)
                op["val"] = dma_count[k]
        self.stats = dict(eng_count=dict(eng_count), n_ops=len(self.ops), n_sems=len(sems))
        per_eng = {e: [op for op in self.ops if op["eng"] == e] for e in ENGS}
        ops = self.ops

        def run(e, h):
            waited = {}
            nwait = 0
            for op in per_eng[e]:
                need = {}
                for d in op["deps"]:
                    A = ops[d]
                    s, v = A["sem"], A["val"]
                    if need.get(s, 0) < v:
                        need[s] = v
                for s, v in need.items():
                    if waited.get(s, 0) < v:
                        h.wait_ge(sems[s], v)
                        waited[s] = v
                        nwait += 1
                if op["dma"] is None:
                    ins = op["fn"](h)
                    if op["signal"]:
                        ins.then_inc(sems[op["sem"]], 1)
                else:
                    op["fn"](h, sems[op["sem"]])
            self.stats["waits_" + e] = nwait

        with nc.Block() as block:
            @block.tensor
            def _(h):
                run("pe", h)

            @block.scalar
            def _(h):
                run("act", h)

            @block.vector
            def _(h):
                run("dve", h)

            @block.gpsimd
            def _(h):
                run("pool", h)

            @block.sync
            def _(h):
                run("sp", h)

    def close(self):
        self.es.close()


class Builder:
    def __init__(self, mode="full", dbg=()):
        self.mode = mode
        self.nc = nc = bass.Bass("TRN2", target_bir_lowering=False)
        self.P = Prog(nc)
        self.D = {}
        self.dbg = dbg
        self.stop_after = None
        self.wreg = {}
        self._uid = 0

    def din(self, name, shape, dt=F32):
        t = self.nc.dram_tensor(name, list(shape), dt, kind="ExternalInput")
        self.D[name] = t
        return t

    def dout(self, name, shape, dt=F32):
        t = self.nc.dram_tensor(name, list(shape), dt, kind="ExternalOutput")
        self.D[name] = t
        return t

    def dscr(self, name, shape, dt):
        t = self.nc.dram_tensor(name, list(shape), dt, kind="Internal")
        self.D[name] = t
        return t

    def uid(self, p="t"):
        self._uid += 1
        return "%s%d" % (p, self._uid)

    def DMA(self, out, in_, key, r=(), w=(), eng="sp"):
        self.P.dma(lambda h, s: h.dma_start(out=out, in_=in_).then_inc(s, 16), key, r, w, eng=eng)

    def MM(self, ps, lhsT, rhs, start, stop, r, w):
        self.P.pe(lambda h: h.matmul(ps, lhsT=lhsT, rhs=rhs, start=start, stop=stop), r, w)

    def ACT(self, out, in_, func, r, w, bias=None, scale=None, accum_out=None):
        kw = {}
        if bias is not None:
            kw["bias"] = bias
        if scale is not None:
            kw["scale"] = scale
        if accum_out is not None:
            kw["accum_out"] = accum_out
        self.P.act(lambda h: h.activation(out=out, in_=in_, func=func, **kw), r, w)

    def ENG(self, eng):
        return {"dve": self.P.dve, "pool": self.P.pool, "act": self.P.act}[eng]

    def CP(self, eng, out, in_, r, w):
        if eng == "act":
            self.P.act(lambda h: h.copy(out=out, in_=in_), r, w)
        else:
            self.ENG(eng)(lambda h: h.tensor_copy(out=out, in_=in_), r, w)

    def TS(self, eng, out, in0, s1, s2, op0, op1, r, w):
        if op1 is None:
            self.ENG(eng)(lambda h: h.tensor_scalar(out=out, in0=in0, scalar1=s1, scalar2=None, op0=op0), r, w)
        else:
            self.ENG(eng)(lambda h: h.tensor_scalar(out=out, in0=in0, scalar1=s1, scalar2=s2, op0=op0, op1=op1), r, w)

    def TTo(self, eng, out, in0, in1, op, r, w):
        self.ENG(eng)(lambda h: h.tensor_tensor(out=out, in0=in0, in1=in1, op=op), r, w)

    def STT(self, out, in0, scalar, in1, op0, op1, r, w, accum_out=None):
        if accum_out is None:
            self.P.dve(lambda h: h.scalar_tensor_tensor(out=out, in0=in0, scalar=scalar, in1=in1, op0=op0, op1=op1), r, w)
        else:
            self.P.dve(lambda h: h.scalar_tensor_tensor(out=out, in0=in0, scalar=scalar, in1=in1, op0=op0, op1=op1,
                                                        accum_out=accum_out), r, w)

    def MEMSET(self, eng, ap, val, w):
        self.ENG(eng)(lambda h: h.memset(ap, val), (), w)

    def wreg_add(self, name, c0, c1, tok):
        self.wreg.setdefault(name, []).append((c0, c1, tok))

    def wcover(self, name, c0, m):
        out = [t for (a, b, t) in self.wreg.get(name, []) if a < c0 + m and b > c0]
        assert out, (name, c0, m)
        return out

    def run_interleaved(self, gen, aux, nsteps):
        per = (len(aux) + nsteps - 1) // max(1, nsteps)
        k = 0
        for _ in gen:
            self.P.replay(aux[k:k + per])
            k += per
        self.P.replay(aux[k:])

    def bank(self, lo=0, hi=8):
        P = self.P
        b = lo + (P.bank_rr % (hi - lo))
        P.bank_rr += 1
        return b

    def setup_globals(self):
        P = self.P
        self.PS = [P.psum("ps%d" % b, [128, 512]) for b in range(8)]
        self.ones_f = P.sbuf("ones_f", [128, 128], F32)
        self.ones_b = P.sbuf("ones_b", [128, 128], BF16)
        self.neg1 = P.sbuf("neg1", [128, 512], F32)
        self.epsv = P.sbuf("epsv", [128, 1], F32)
        self.invf = P.sbuf("invf_sb", [128, 1], F32)
        self.cbig = P.sbuf("cbig_sb", [128, 896], BF16)
        self.wbig = P.sbuf("wbig_sb", [128, 896], BF16)
        self.MEMSET("dve", self.ones_f[:], 1.0, ["ones_f"])
        self.ones_r = P.sbuf("ones_r", [128, 128], F32R)
        self.CP("dve", self.ones_r[:], self.ones_f[:], ["ones_f"], ["ones_r"])
        self.MEMSET("dve", self.ones_b[:], 1.0, ["ones_b"])
        self.MEMSET("pool", self.neg1[:], -1.0, ["neg1"])
        self.MEMSET("dve", self.epsv[:], 1e-6, ["epsv"])
        self.DMA(self.invf[:], self.D["invf"].ap(), "g_invf", w=["invf"])
        self.identb = P.sbuf("identb", [128, 128], BF16)
        P.phase_begin()
        tmp = P.palloc("cb_tmp", [128, 896], F32)
        tmp2 = P.palloc("id_tmp", [128, 128], F32)
        self.DMA(tmp[:], self.D["cbig"].ap(), "g_cb", w=["cb_tmp"])
        self.DMA(tmp2[:], self.D["ident"].ap(), "g_id", w=["id_tmp"])
        self.CP("dve", self.identb[:], tmp2[:], ["id_tmp"], ["identb"])
        self.TS("dve", self.cbig[:], tmp[:], 30000.0, -30000.0, ALU.mult, ALU.add, ["cb_tmp"], ["cbig"])
        self.TS("dve", self.wbig[:], tmp[:], -30000.0, None, ALU.mult, None, ["cb_tmp"], ["wbig"])
        P.phase_end()

    def load_w(self, dst, src3, kcn, n, stg, dstname):
        engs = ["dve", "act"]
        for c0 in range(0, n, 256):
            m = min(256, n - c0)
            sl = self._wst % len(stg)
            e = engs[self._wst % 2]
            self._wst += 1
            self.DMA(stg[sl][:, 0:kcn, 0:m], src3[:, :, c0:c0 + m], ("wst", sl), w=[("wst", sl)])
            self.CP(e, dst[:, 0:kcn, c0:c0 + m], stg[sl][:, 0:kcn, 0:m], [("wst", sl)], [(dstname, c0)])
            self.wreg_add(dstname, c0, c0 + m, (dstname, c0))

    def make_rot(self, W, kcn, c0, r0, tok_src, tok_dst):
        self.TS("dve", W[:, 0:kcn, r0:r0 + 32], W[:, 0:kcn, c0 + 32:c0 + 64], -1.0, None, ALU.mult, None,
                tok_src, [tok_dst + ("a",)])
        self.CP("pool", W[:, 0:kcn, r0 + 32:r0 + 64], W[:, 0:kcn, c0:c0 + 32], tok_src, [tok_dst + ("b",)])
        self.wreg_add(tok_dst[0], r0, r0 + 32, tok_dst + ("a",))
        self.wreg_add(tok_dst[0], r0 + 32, r0 + 64, tok_dst + ("b",))

    def rope_tables(self, i, T, src=None, n=TT, sb_src=False):
        npart = T["npart"]
        if src is None:
            src = bass.AP(self.D["pos"], TT * i, [[0, npart], [1, TT]])
        ti, f0, f1, f2, f3 = (T[k] for k in ("ti", "f0", "f1", "f2", "f3"))
        slot = i % len(T["cos"])
        cs, sn = T["cos"][slot], T["sin"][slot]
        if sb_src:
            self.CP("dve", ti[:, 0:n], src, ["posfull"], ["rt_ti"])
        else:
            self.DMA(ti[:, 0:n], src, "rt_ti", w=["rt_ti"])
        self.CP("dve", f0[:, 0:n], ti[:, 0:n], ["rt_ti"], ["rt_f0"])
        self.TS("dve", f1[:, 0:n], f0[:, 0:n], self.invf[0:npart, 0:1], None, ALU.mult, None, ["rt_f0", "invf"], ["rt_f1"])
        self.TS("dve", f0[:, 0:n], f1[:, 0:n], float(1.0 / (2 * math.pi)), 0.5, ALU.mult, ALU.add, ["rt_f1"], ["rt_f0"])
        self.CP("dve", ti[:, 0:n], f0[:, 0:n], ["rt_f0"], ["rt_ti"])
        self.CP("dve", f0[:, 0:n], ti[:, 0:n], ["rt_ti"], ["rt_f0"])
        C1 = 6.28125
        C2 = float(np.float32(2 * math.pi - 6.28125))
        self.STT(f2[:, 0:n], f0[:, 0:n], -C1, f1[:, 0:n], ALU.mult, ALU.add, ["rt_f0", "rt_f1"], ["rt_f2"])
        self.STT(f2[:, 0:n], f0[:, 0:n], -C2, f2[:, 0:n], ALU.mult, ALU.add, ["rt_f0", "rt_f2"], ["rt_f2"])
        B = 3.1415925
        TWO_PI = 2 * math.pi

        def wrap(dst, shift, tag):
            self.TS("dve", dst[:, 0:n], f2[:, 0:n], float(shift), None, ALU.add, None, ["rt_f2"], [tag])
            self.TS("dve", f0[:, 0:n], dst[:, 0:n], math.pi, -TWO_PI, ALU.is_gt, ALU.mult, [tag], ["rt_f0"])
            self.TTo("dve", dst[:, 0:n], dst[:, 0:n], f0[:, 0:n], ALU.add, [tag, "rt_f0"], [tag])
            self.TS("dve", f0[:, 0:n], dst[:, 0:n], -math.pi, TWO_PI, ALU.is_lt, ALU.mult, [tag], ["rt_f0"])
            self.TTo("dve", dst[:, 0:n], dst[:, 0:n], f0[:, 0:n], ALU.add, [tag, "rt_f0"], [tag])
            self.TS("dve", dst[:, 0:n], dst[:, 0:n], B, -B, ALU.min, ALU.max, [tag], [tag])

        wrap(f1, 0.0, "rt_f1")
        self.ACT(sn[:, 0:n], f1[:, 0:n], AF.Sin, ["rt_f1"], [("rt_sin", slot)])
        wrap(f3, math.pi / 2, "rt_f3")
        self.ACT(cs[:, 0:n], f3[:, 0:n], AF.Sin, ["rt_f3"], [("rt_cos", slot)])
        return cs, sn, [("rt_cos", slot), ("rt_sin", slot)]

    def alloc_rope_scratch(self, npart=64, nslots=1):
        P = self.P
        T = {"npart": npart}
        T["ti"] = P.palloc("rt_ti", [npart, TT], I32)
        for k in ("f0", "f1", "f2", "f3"):
            T[k] = P.palloc("rt_" + k, [npart, TT], F32)
        T["cos"] = [P.palloc("rt_cos%d" % k, [npart, TT], F32) for k in range(nslots)]
        T["sin"] = [P.palloc("rt_sin%d" % k, [npart, TT], F32) for k in range(nslots)]
        T["t1"] = [P.palloc("rp_t1_%d" % s, [npart, TT], F32) for s in range(1)]
        T["t2"] = [P.palloc("rp_t2_%d" % s, [npart, TT], F32) for s in range(1)]
        self._rp = 0
        return T

    def rope_apply(self, out, psa, psb, cs, sn, T, r, w, n=TT, npart=None):
        t1, t2 = T["t1"][0], T["t2"][0]
        np_ = npart if npart is not None else 64
        self.TTo("dve", t1[0:np_, 0:n], psa, cs[0:np_, 0:n], ALU.mult, r, ["rp1"])
        self.TTo("dve", t2[0:np_, 0:n], psb, sn[0:np_, 0:n], ALU.mult, r, ["rp2"])
        self.TTo("pool", out, t1[0:np_, 0:n], t2[0:np_, 0:n], ALU.add, ["rp1", "rp2"], w)

    def rstd_tile(self, xt, kcn, inv_n, T, rtok):
        b = 7
        psN = self.PS[b]
        for c in range(kcn):
            sl = self._sq % 2
            self._sq += 1
            sq = T["sq"][sl]
            self.ACT(sq[:], xt[:, c, :], AF.Square, rtok, [("sq", sl)])
            self.MM(psN[:], self.ones_r[:], sq[:], c == 0, c == kcn - 1, ["ones_r", ("sq", sl)], [("ps", b)])
        lnt = T["lnt"]
        rstd = T["rstd"]
        self.ACT(lnt[:], psN[:], AF.Ln, [("ps", b), "epsv"], ["lnt"], bias=self.epsv[:, 0:1], scale=float(inv_n))
        self.ACT(rstd[:], lnt[:], AF.Exp, ["lnt"], ["rstd"], scale=-0.5)
        return rstd

    def alloc_norm_scratch(self):
        P = self.P
        T = {}
        T["sq"] = [P.palloc("sq%d" % s, [128, TT], F32R) for s in range(2)]
        T["lnt"] = P.palloc("lnt", [128, TT], F32)
        T["rstd"] = P.palloc("rstd", [128, TT], F32)
        self._sq = 0
        return T

    def norm_apply(self, out, xt, kcn, gvec, rstd, rtok, wtok, gtok):
        for c in range(kcn):
            self.STT(out[:, c, :], xt[:, c, :], gvec[:, c:c + 1], rstd[:], ALU.mult, ALU.mult,
                     list(rtok) + ["rstd", gtok], [wtok + (c,)])

    def phase_C1(self):
        P = self.P
        D = self.D
        P.phase_begin()
        self._wst = 0
        Wc = P.palloc("Wc", [128, 8, 1664], BF16)
        Wuq = P.palloc("Wuq", [128, 2, 2048], BF16)
        Wukv = P.palloc("Wukv", [128, 2, 2048], BF16)
        stg = [P.palloc("wstg%d" % s, [128, 8, 256], F32) for s in range(4)]
        cn = P.palloc("cn", [128, 8], F32)
        qn_g = P.palloc("qn_g", [128, 2], F32)
        kvn_g = P.palloc("kvn_g", [128, 2], F32)
        self.DMA(cn[:], D["c_norm"].ap(), "cn", w=["cn"])
        self.DMA(qn_g[:], D["c_q_norm"].ap(), "qn_g", w=["qn_g"])
        self.DMA(kvn_g[:], D["c_kv_norm"].ap(), "kvn_g", w=["kvn_g"])
        NS = self.alloc_norm_scratch()
        RS = self.alloc_rope_scratch(64, 2)
        xt = [P.palloc("xt%d" % s, [128, 8, TT], F32) for s in range(2)]
        xn2 = [P.palloc("xn%d" % s_, [128, 8, TT], BF16) for s_ in range(2)]
        cqf = P.palloc("cqf", [128, 2, TT], F32)
        ckvf = P.palloc("ckvf", [128, 2, TT], F32)
        cqn = P.palloc("cqn", [128, 2, TT], BF16)
        ckvn = P.palloc("ckvn", [128, 2, TT], BF16)
        kpe_s = P.palloc("kpe_s", [64, TT], BF16)
        gst = P.palloc("gst", [128, 8, TT], BF16)
        qst = P.palloc("qst", [128, 8, TT], BF16)
        qrst = P.palloc("qrst", [64, 8, TT], BF16)
        kst = P.palloc("kst", [128, 8, TT], BF16)
        vst = [P.palloc("vst%d" % s, [128, 4, 128], BF16) for s in range(2)]

        x1v = D["x1T"].ap().rearrange("(c p) t -> p c t", p=128)
        gTv = D["gT1"].ap().rearrange("(c p) t -> p c t", p=128)
        qnv = D["qn"].ap().rearrange("h p t -> p h t")
        qrv = D["qr"].ap().rearrange("h p t -> p h t")
        knv = D["kn"].ap().rearrange("h p t -> p h t")

        def load_x(i):
            sl = i % 2
            self.DMA(xt[sl][:], x1v[:, :, i * TT:(i + 1) * TT], ("xt", sl), w=[("xt", sl)])

        load_x(0)
        load_x(1)
        evs = ["act", "dve"]
        nev = 0

        def pre(i):
            sl = i % 2
            xtok = [("xt", sl)]
            rstd = self.rstd_tile(xt[sl], 8, 1.0 / DM, NS, xtok)
            self.norm_apply(xn2[sl], xt[sl], 8, cn, rstd, xtok, ("xn", sl), "cn")

        def body(i):
            nonlocal nev
            sl = i % 2
            xn = xn2[sl]
            tsl = slice(i * TT, (i + 1) * TT)
            xnR = [("xn", sl, c) for c in range(8)]
            cs, sn, rtoks = ropeT[i]

            def proj8(ps_ap, col0, m, wtok):
                for kc in range(8):
                    self.MM(ps_ap, Wc[:, kc, col0:col0 + m], xn[:, kc, :], kc == 0, kc == 7, self.wcover("Wc", col0, m) + xnR, wtok)

            for (dstf, col0, nm) in ((cqf, 0, "cqf"), (ckvf, 256, "ckvf")):
                for j in range(2):
                    b = self.bank(0, 7)
                    proj8(self.PS[b][:], col0 + 128 * j, 128, [("ps", b)])
                    self.CP("act", dstf[:, j, :], self.PS[b][:], [("ps", b)], [(nm, j)])
            for (dstf, dstn, gv, nm, gtok) in ((cqf, cqn, qn_g, "cqf", "qn_g"), (ckvf, ckvn, kvn_g, "ckvf", "kvn_g")):
                ftok = [(nm, 0), (nm, 1)]
                rstd = self.rstd_tile(dstf, 2, 1.0 / 256, NS, ftok)
                self.norm_apply(dstn, dstf, 2, gv, rstd, ftok, (nm + "n",), gtok)
            cqnR = [("cqfn", 0), ("cqfn", 1)]
            ckvnR = [("ckvfn", 0), ("ckvfn", 1)]
            ba = self.bank(0, 7)
            bb = self.bank(0, 7)
            proj8(self.PS[ba][0:64, :], 512, 64, [("ps", ba)])
            proj8(self.PS[bb][0:64, :], 1600, 64, [("ps", bb)])
            self.rope_apply(kpe_s[:], self.PS[ba][0:64, :], self.PS[bb][0:64, :], cs, sn, RS,
                            [("ps", ba), ("ps", bb)] + rtoks, ["kpe_s"])
            self.DMA(D["kpe"].ap()[:, tsl], kpe_s[:], "kpe_s", r=["kpe_s"], w=[("kpe_d", i)])
            for c in range(8):
                b = self.bank(0, 7)
                proj8(self.PS[b][:], 576 + 128 * c, 128, [("ps", b)])
                self.ACT(gst[:, c, :], self.PS[b][:], AF.Silu, [("ps", b)], [("gst", c)])
            self.DMA(gTv[:, :, tsl], gst[:], "gst", r=[("gst", c) for c in range(8)], w=[("gT_d", i)])
            yield
            for h in range(8):
                b = self.bank(0, 7)
                for kc in range(2):
                    self.MM(self.PS[b][:], Wuq[:, kc, h * 192:h * 192 + 128], cqn[:, kc, :], kc == 0, kc == 1,
                            self.wcover("Wuq", h * 192, 128) + cqnR, [("ps", b)])
                self.CP(evs[nev % 2], qst[:, h, :], self.PS[b][:], [("ps", b)], [("qst", h)])
                nev += 1
                ba = self.bank(0, 7)
                bb = self.bank(0, 7)
                for kc in range(2):
                    self.MM(self.PS[ba][0:64, :], Wuq[:, kc, h * 192 + 128:h * 192 + 192], cqn[:, kc, :], kc == 0, kc == 1,
                            self.wcover("Wuq", h * 192 + 128, 64) + cqnR, [("ps", ba)])
                for kc in range(2):
                    self.MM(self.PS[bb][0:64, :], Wuq[:, kc, 1536 + 64 * h:1600 + 64 * h], cqn[:, kc, :], kc == 0, kc == 1,
                            self.wcover("Wuq", 1536 + 64 * h, 64) + cqnR, [("ps", bb)])
                self.rope_apply(qrst[:, h, :], self.PS[ba][0:64, :], self.PS[bb][0:64, :], cs, sn, RS,
                                [("ps", ba), ("ps", bb)] + rtoks, [("qrst", h)])
                b = self.bank(0, 7)
                for kc in range(2):
                    self.MM(self.PS[b][:], Wukv[:, kc, h * 256:h * 256 + 128], ckvn[:, kc, :], kc == 0, kc == 1,
                            self.wcover("Wukv", h * 256, 128) + ckvnR, [("ps", b)])
                self.CP(evs[nev % 2], kst[:, h, :], self.PS[b][:], [("ps", b)], [("kst", h)])
                nev += 1
                b = self.bank(0, 7)
                for st in range(4):
                    for kc in range(2):
                        self.MM(self.PS[b][:, st * 128:(st + 1) * 128], ckvn[:, kc, st * 128:(st + 1) * 128],
                                Wukv[:, kc, h * 256 + 128:h * 256 + 256], kc == 0, kc == 1,
                                self.wcover("Wukv", h * 256 + 128, 128) + ckvnR, [("ps", b)])
                vs = h % 2
                self.CP(evs[nev % 2], vst[vs][:].rearrange("p a d -> p (a d)"), self.PS[b][:], [("ps", b)], [("vst", vs)])
                nev += 1
                vdst = D["vS"].ap()[h].rearrange("(t p) d -> p t d", p=128)[:, 4 * i:4 * i + 4, :]
                self.DMA(vdst, vst[vs][:], ("vst", vs), r=[("vst", vs)], w=[("v_d", h, i)])
                yield
            self.DMA(qnv[:, :, tsl], qst[:], "qst", r=[("qst", h) for h in range(8)], w=[("qn_d", i)])
            self.DMA(qrv[:, :, tsl], qrst[:], "qrst", r=[("qrst", h) for h in range(8)], w=[("qr_d", i)])
            self.DMA(knv[:, :, tsl], kst[:], "kst", r=[("kst", h) for h in range(8)], w=[("kn_d", i)])
        pre(0)
        self.load_w(Wc, D["c_w_in"].ap().rearrange("(c p) n -> p c n", p=128), 8, 1600, stg, "Wc")
        self.load_w(Wuq, D["c_w_uq"].ap().rearrange("(c p) n -> p c n", p=128), 2, 1536, stg, "Wuq")
        self.load_w(Wukv, D["c_w_ukv"].ap().rearrange("(c p) n -> p c n", p=128), 2, 2048, stg, "Wukv")
        self.make_rot(Wc, 8, 512, 1600, self.wcover("Wc", 512, 64), ("Wc", "rot"))
        for h in range(8):
            c0 = h * 192 + 128
            self.make_rot(Wuq, 2, c0, 1536 + 64 * h, self.wcover("Wuq", c0, 64), ("Wuq", "rot", h))
        WcR = [("Wc", c) for c in range(0, 1600, 256)] + [("Wc", "rot", "a"), ("Wc", "rot", "b")]
        WuqR = [("Wuq", c) for c in range(0, 1536, 256)] + [("Wuq", "rot", h, x) for h in range(8) for x in "ab"]
        WukvR = [("Wukv", c) for c in range(0, 2048, 256)]

        ropeT = {0: self.rope_tables(0, RS)}
        for i in range(NT):
            gen = body(i)
            next(gen)
            if i + 2 < NT:
                load_x(i + 2)
            aux = []
            if i + 1 < NT:
                P.defer_begin()
                pre(i + 1)
                ropeT[i + 1] = self.rope_tables(i + 1, RS)
                aux = P.defer_end()
            self.run_interleaved(gen, aux, 8)
        P.phase_end()

    def phase_C2(self):
        P = self.P
        D = self.D
        P.phase_begin()
        kpe = P.palloc("kpe_sbuf", [128, S], BF16)
        self.DMA(kpe[0:64, :], D["kpe"].ap(), "kpe_sb", w=["kpe_sb"])
        self.DMA(kpe[64:128, :], D["kpe"].ap(), "kpe_sb2", w=["kpe_sb2"])
        Kn = [P.palloc("Kn%d" % s, [128, S], BF16) for s in range(2)]
        Vh = [P.palloc("Vh%d" % s, [128, 32, 128], BF16) for s in range(2)]
        NQ = 3
        qn_t = [P.palloc("qn_t%d" % s, [128, TT], BF16) for s in range(NQ)]
        qr_t = [P.palloc("qr_t%d" % s, [128, TT], BF16) for s in range(NQ)]
        NPT = 8
        pt = [P.palloc("pt%d" % s, [128, TT], BF16) for s in range(NPT)]
        accD = [P.palloc("accD%d" % s, [128, TT], F32) for s in range(2)]
        accP = [P.palloc("accP%d" % s, [128, TT], F32) for s in range(2)]
        lsb = [P.palloc("lsb%d" % s, [128, TT], F32) for s in range(2)]
        rinv = [P.palloc("rinv%d" % s, [128, TT], F32) for s in range(2)]
        ost = [P.palloc("ost%d" % s, [128, TT], BF16) for s in range(2)]
        scale = float((128 + 64) ** -0.5)
        cnt = dict(pt=0, q=0, ep=0, sb=0)

        def load_head(h):
            hs = h % 2
            self.DMA(Kn[hs][:], D["kn"].ap()[h], ("Kn", hs), w=[("Kn", hs)])
            self.DMA(Vh[hs][:], D["vS"].ap()[h].rearrange("(t p) d -> p t d", p=128), ("Vh", hs), w=[("Vh", hs)])

        def load_q(h, i):
            qs = cnt["q"] % NQ
            cnt["q"] += 1
            tsl = slice(i * TT, (i + 1) * TT)
            self.DMA(qn_t[qs][:], D["qn"].ap()[h][:, tsl], ("qn_t", qs), w=[("qn_t", qs)])
            self.DMA(qr_t[qs][0:64, :], D["qr"].ap()[h][:, tsl], ("qr_t", qs), w=[("qr_t", qs)])
            self.DMA(qr_t[qs][64:128, :], D["qr"].ap()[h][:, tsl], ("qr_t2", qs), w=[("qr_t2", qs)])
            return qs

        seq = [(h, i) for h in range(8) for i in range(NT)]
        load_head(0)
        qslots = {}
        qslots[seq[0]] = load_q(*seq[0])
        stream = []
        for n, (h, i) in enumerate(seq):
            nk = 4 * i + 4
            for kt in range(0, nk, 2):
                stream.append((n, h, i, kt, nk))
        state = {}

        def front(n, h, i, kt0, nk):
            if kt0 == 0:
                if i == 1 and h + 1 < 8:
                    load_head(h + 1)
                if n + 1 < len(seq):
                    qslots[seq[n + 1]] = load_q(*seq[n + 1])
                ep = cnt["ep"] % 2
                cnt["ep"] += 1
                state[n] = dict(ep=ep, pts={})
            hs = h % 2
            qs = qslots[(h, i)]
            banks = []
            c0s = [128 * max(0, kt0 + u - 4 * i) for u in range(2)]
            for u in range(2):
                bS = cnt["sb"] % 4
                cnt["sb"] += 1
                banks.append(bS)
                ksl = slice((kt0 + u) * 128, (kt0 + u + 1) * 128)
                self.MM(self.PS[bS][:, c0s[u]:], Kn[hs][:, ksl], qn_t[qs][:, c0s[u]:], True, False,
                        [("Kn", hs), ("qn_t", qs)], [("ps", bS)])
            for u in range(2):
                kt = kt0 + u
                bS = banks[u]
                ksl = slice(kt * 128, (kt + 1) * 128)
                rows = slice(64 * u, 64 * u + 64)
                diag = kt >= 4 * i
                self.MM(self.PS[bS][:, c0s[u]:], kpe[rows, ksl], qr_t[qs][rows, c0s[u]:], False, not diag,
                        ["kpe_sb", "kpe_sb2", ("qr_t", qs), ("qr_t2", qs)], [("ps", bS)])
            for u in range(2):
                kt = kt0 + u
                bS = banks[u]
                if kt >= 4 * i:
                    self.MM(self.PS[bS][:, c0s[u]:c0s[u] + 128], self.identb[:], self.cbig[:, 384:512], False, True,
                            ["identb", "cbig"], [("ps", bS)])
            for u in range(2):
                kt = kt0 + u
                bS = banks[u]
                ps_ = cnt["pt"] % NPT
                cnt["pt"] += 1
                self.ACT(pt[ps_][:, c0s[u]:], self.PS[bS][:, c0s[u]:], AF.Exp, [("ps", bS)], [("pt", ps_)], scale=scale)
                state[n]["pts"][kt] = ps_

        def back(n, h, i, kt0, nk):
            hs = h % 2
            ep = state[n]["ep"]
            bO = 4 + ep
            bL = 6 + ep
            tsl = slice(i * TT, (i + 1) * TT)
            for u in range(2):
                kt = kt0 + u
                ps_ = state[n]["pts"][kt]
                cc0 = 128 * max(0, kt - 4 * i)
                self.MM(self.PS[bO][:, cc0:], Vh[hs][:, kt, :], pt[ps_][:, cc0:], kt == 0, kt == nk - 1,
                        [("Vh", hs), ("pt", ps_)], [("ps", bO)])
                if kt == 0:
                    self.CP("dve", accD[ep][:], pt[ps_][:], [("pt", ps_)], [("accD", ep)])
                else:
                    c0 = 128 * max(0, kt - 4 * i)
                    self.TTo("dve", accD[ep][:, c0:], accD[ep][:, c0:], pt[ps_][:, c0:], ALU.add,
                             [("pt", ps_), ("accD", ep)], [("accD", ep)])
            if kt0 + 2 >= nk:
                def epilogue(ep=ep, bO=bO, bL=bL, h=h, tsl=tsl, i=i):
                    self.MM(self.PS[bL][:], self.ones_f[:], accD[ep][:], True, True, ["ones_f", ("accD", ep)], [("ps", bL)])
                    self.ACT(lsb[ep][:], self.PS[bL][:], AF.Ln, [("ps", bL)], [("lsb", ep)])
                    self.ACT(rinv[ep][:], lsb[ep][:], AF.Exp, [("lsb", ep)], [("rinv", ep)], scale=-1.0)
                    self.TTo("dve", ost[ep][:], self.PS[bO][:], rinv[ep][:], ALU.mult, [("ps", bO), ("rinv", ep)], [("ost", ep)])
                    self.DMA(D["oT"].ap()[h * 128:(h + 1) * 128, tsl], ost[ep][:], ("ost", ep), r=[("ost", ep)],
                             w=[("oT_d", h, i)])
                pending.append([2, epilogue])

        LA = 1
        pending = []
        for idx in range(len(stream) + LA):
            if idx < len(stream):
                front(*stream[idx])
            if idx - LA >= 0:
                back(*stream[idx - LA])
            for pe_ in list(pending):
                pe_[0] -= 1
                if pe_[0] < 0:
                    pe_[1]()
                    pending.remove(pe_)
        for pe_ in pending:
            pe_[1]()
        P.phase_end()

    def phase_C3(self):
        P = self.P
        D = self.D
        P.phase_begin()
        self._wst = 0
        Wo = P.palloc("Wo", [128, 8, 1024], BF16)
        stg = [P.palloc("wstg%d" % s, [128, 8, 256], F32) for s in range(4)]
        fn = P.palloc("fn", [128, 8], F32)
        self.DMA(fn[:], D["final_norm"].ap(), "fn", w=["fn"])
        WoR = [("Wo", c) for c in range(0, 1024, 256)]
        NS = self.alloc_norm_scratch()
        og = [P.palloc("og%d" % s, [128, 8, TT], BF16) for s in range(2)]
        gt = [P.palloc("gt%d" % s, [128, 8, TT], BF16) for s in range(2)]
        x1 = [P.palloc("x1_%d" % s, [128, 8, TT], F32) for s in range(2)]
        mm = P.palloc("mm", [128, 8, TT], BF16)
        x2 = P.palloc("x2", [128, 8, TT], F32)
        ot = P.palloc("ot", [128, 8, TT], F32)
        oTv = D["oT"].ap().rearrange("(c p) t -> p c t", p=128)
        gTv = D["gT1"].ap().rearrange("(c p) t -> p c t", p=128)
        x1v = D["x1T"].ap().rearrange("(c p) t -> p c t", p=128)
        outv = D["outT"].ap().rearrange("(c p) t -> p c t", p=128)

        def load(i):
            sl = i % 2
            tsl = slice(i * TT, (i + 1) * TT)
            self.DMA(og[sl][:], oTv[:, :, tsl], ("og", sl), w=[("og", sl)])
            self.DMA(gt[sl][:], gTv[:, :, tsl], ("gt", sl), w=[("gt", sl)])
            self.DMA(x1[sl][:], x1v[:, :, tsl], ("x1", sl), w=[("x1", sl)])

        load(0)
        self.load_w(Wo, D["c_w_out"].ap().rearrange("(c p) n -> p c n", p=128), 8, 1024, stg, "Wo")
        for i in range(NT):
            if i + 1 < NT:
                load(i + 1)
            sl = i % 2
            tsl = slice(i * TT, (i + 1) * TT)
            self.TTo("pool", mm[:], og[sl][:], gt[sl][:], ALU.mult, [("og", sl), ("gt", sl)], ["mm"])
            for dc in range(8):
                b = self.bank(0, 7)
                for kc in range(8):
                    self.MM(self.PS[b][:], Wo[:, kc, dc * 128:(dc + 1) * 128], mm[:, kc, :], kc == 0, kc == 7,
                            self.wcover("Wo", dc * 128, 128) + ["mm"], [("ps", b)])
                self.TTo("dve", x2[:, dc, :], self.PS[b][:], x1[sl][:, dc, :], ALU.add, [("ps", b), ("x1", sl)], [("x2", dc)])
            x2R = [("x2", dc) for dc in range(8)]
            rstd = self.rstd_tile(x2, 8, 1.0 / DM, NS, x2R)
            self.norm_apply(ot, x2, 8, fn, rstd, x2R, ("ot",), "fn")
            self.DMA(outv[:, :, tsl], ot[:], "ot", r=[("ot", c) for c in range(8)], w=[("out_d", i)])
        P.phase_end()

    def layer0(self):
        P = self.P
        D = self.D
        L = ExitStack()
        nc = self.nc

        def lal(name, shape, dt):
            return L.enter_context(nc.sbuf_tensor(name, list(shape), dt))

        R = self.R0 = {}
        R["KE"] = [lal("KE%d" % g, [128, S], BF16) for g in range(2)]
        R["KW"] = [lal("KW%d" % g, [64, S], BF16) for g in range(2)]
        R["Vs"] = lal("Vs", [128, 2, 32, 128], BF16)
        R["Vw"] = lal("Vw", [128, 2, 32, 128], BF16)
        R["KC"] = [lal("KC%d" % g, [64, 256], BF16) for g in range(2)]
        R["Vc"] = lal("Vc", [128, 2, 2, 128], BF16)
        R["SB"] = lal("SB", [128, S], BF16)
        R["tbig"] = lal("tbig_sb", [128, S], BF16)
        self.KCR = {"KCR": lal("KCR", [128, S], BF16), "VCR": lal("VCR", [128, S], BF16)}
        for nm in ("A1", "A1b", "A2", "A2b", "A3", "A4"):
            getattr(self, "phase_" + nm)()
            if self.stop_after == nm:
                break
        L.close()

    def load_seg(self, dst, dcol, src_w, scol, n, kcn, stg_ap_fn, tokname, chunk=512):
        engs = ["dve", "act"]
        for c0 in range(0, n, chunk):
            m = min(chunk, n - c0)
            e = engs[self._wst % 2]
            self._wst += 1
            stg, stok = stg_ap_fn()
            self.DMA(stg[:, 0:kcn, 0:m], src_w[:, :, scol + c0:scol + c0 + m], stok, w=[stok])
            self.CP(e, dst[:, 0:kcn, dcol + c0:dcol + c0 + m], stg[:, 0:kcn, 0:m], [stok], [(tokname, dcol + c0)])
            self.wreg_add(tokname, dcol + c0, dcol + c0 + m, (tokname, dcol + c0))

    def phase_A1(self):
        P = self.P
        D = self.D
        R = self.R0
        P.phase_begin()
        self._wst = 0
        NA = 2072
        Wa = P.palloc("Wa", [128, 8, NA], BF16)
        xt = P.palloc("xt", [128, 8, TT], F32)
        an = P.palloc("an", [128, 8], F32)
        self.DMA(an[:], D["a_norm"].ap(), "an", w=["an"])
        src = D["a_w_in"].ap().rearrange("(c p) n -> p c n", p=128)
        segs = [(0, 0, 512), (1024, 768, 128), (1280, 1024, 128), (1536, 512, 256), (1792, 896, 128),
                (1920, 1152, 128), (2048, 1280, 24)]
        nst = [0]

        def stgfn():
            k = nst[0] % 2
            nst[0] += 1
            return xt[:, :, 256 * k:256 * k + 256], ("xtH", k)

        for (dc, sc, n) in segs:
            self.load_seg(Wa, dc, src, sc, n, 8, stgfn, "Wa", chunk=256)
        WaR = [("Wa", dc) for (dc, sc, n) in segs]
        for h in range(8):
            self.make_rot(Wa, 8, 64 * h, 512 + 64 * h, self.wcover("Wa", 64 * h, 64), ("Wa", "rq", h))
            WaR += [("Wa", "rq", h, "a"), ("Wa", "rq", h, "b")]
        for g in range(2):
            self.make_rot(Wa, 8, 1024 + 64 * g, 1152 + 64 * g, self.wcover("Wa", 1024 + 64 * g, 64), ("Wa", "rs", g))
            self.make_rot(Wa, 8, 1280 + 64 * g, 1408 + 64 * g, self.wcover("Wa", 1280 + 64 * g, 64), ("Wa", "rw", g))
            WaR += [("Wa", "rs", g, "a"), ("Wa", "rs", g, "b"), ("Wa", "rw", g, "a"), ("Wa", "rw", g, "b")]
        NS = self.alloc_norm_scratch()
        RS = self.alloc_rope_scratch(128)
        xn2 = [P.palloc("xn%d" % s_, [128, 8, TT], BF16) for s_ in range(2)]
        qst = P.palloc("qst", [128, 4, TT], BF16)
        kstg = [P.palloc("kstg%d" % s_, [128, TT], BF16) for s_ in range(2)]
        gsb = P.palloc("gsb", [24, TT], F32)
        xv = D["xT"].ap().rearrange("(c p) t -> p c t", p=128)
        q0v = D["q0"].ap().rearrange("(pr two) p t -> (two p) pr t", two=2)
        evs = ["act", "dve"]
        nev = 0
        def pre(i):
            tsl = slice(i * TT, (i + 1) * TT)
            self.DMA(xt[:], xv[:, :, tsl], "xt", w=["xt", ("xtH", 0), ("xtH", 1)])
            rstd = self.rstd_tile(xt, 8, 1.0 / DM, NS, ["xt"])
            self.norm_apply(xn2[i % 2], xt, 8, an, rstd, ["xt"], ("xn", i % 2), "an")

        def body(i):
            nonlocal nev
            tsl = slice(i * TT, (i + 1) * TT)
            xn = xn2[i % 2]
            xnR = [("xn", i % 2, c) for c in range(8)]
            cs, sn, rtoks = ropeT[i]

            def proj8(ps_ap, col0, m, wtok):
                for kc in range(8):
                    self.MM(ps_ap, Wa[:, kc, col0:col0 + m], xn[:, kc, :], kc == 0, kc == 7, self.wcover("Wa", col0, m) + xnR, wtok)

            def roped(dst, c_main, c_rot, wtok):
                ba = self.bank(0, 7)
                bb = self.bank(0, 7)
                proj8(self.PS[ba][:, :], c_main, 128, [("ps", ba)])
                proj8(self.PS[bb][:, :], c_rot, 128, [("ps", bb)])
                self.rope_apply(dst, self.PS[ba][:, :], self.PS[bb][:, :], cs, sn, RS,
                                [("ps", ba), ("ps", bb)] + rtoks, wtok, npart=128)

            for pr in range(4):
                roped(qst[:, pr, :], 128 * pr, 512 + 128 * pr, [("qst", pr)])
            self.DMA(q0v[:, :, tsl], qst[:], "qst", r=[("qst", pr) for pr in range(4)], w=[("q0_d", i)])
            for kk_, (dsts, cm, cr) in enumerate(((R["KE"], 1024, 1152), (R["KW"], 1280, 1408))):
                roped(kstg[kk_][:], cm, cr, [("kstg", kk_)])
                self.CP("pool", dsts[0][0:64, tsl], kstg[kk_][0:64, :], [("kstg", kk_)], [("Kd", kk_, 0, i)])
                self.CP("act", dsts[1][0:64, tsl], kstg[kk_][64:128, :], [("kstg", kk_)], [("Kd", kk_, 1, i)])
            yield
            for (nm, col) in (("KCR", 1536), ("VCR", 1664)):
                b = self.bank(0, 7)
                proj8(self.PS[b][:], col, 128, [("ps", b)])
                self.CP(evs[nev % 2], self.KCR[nm][:, tsl], self.PS[b][:], [("ps", b)], [(nm, i)])
                nev += 1
                yield
            for st in range(4):
                b = self.bank(0, 7)
                for kc in range(8):
                    self.MM(self.PS[b][:, 0:256], xn[:, kc, st * 128:(st + 1) * 128], Wa[:, kc, 1792:2048], kc == 0, kc == 7,
                            self.wcover("Wa", 1792, 256) + xnR, [("ps", b)])
                kt = 4 * i + st
                for g in range(2):
                    self.CP("act", R["Vs"][:, g, kt, 0:64], self.PS[b][:, 64 * g:64 * g + 64], [("ps", b)], [("Vs", g, kt)])
                    self.CP("act", R["Vw"][:, g, kt, 0:64], self.PS[b][:, 128 + 64 * g:192 + 64 * g], [("ps", b)], [("Vw", g, kt)])
                yield
            b = self.bank(0, 7)
            proj8(self.PS[b][0:24, :], 2048, 24, [("ps", b)])
            self.ACT(gsb[:], self.PS[b][0:24, :], AF.Sigmoid, [("ps", b)], ["gsb"])
            self.DMA(D["gTa"].ap()[:, tsl], gsb[:], "gsb", r=["gsb"], w=[("gTa_d", i)])
        pre(0)
        ropeT = {0: self.rope_tables(0, RS)}
        for i in range(NT):
            gen = body(i)
            next(gen)
            aux = []
            if i + 1 < NT:
                P.defer_begin()
                pre(i + 1)
                ropeT[i + 1] = self.rope_tables(i + 1, RS)
                aux = P.defer_end()
            self.run_interleaved(gen, aux, 6)
        P.phase_end()

    def phase_A1b(self):
        P = self.P
        D = self.D
        P.phase_begin()
        self._wst = 0
        Wb = P.palloc("Wb", [128, 8, 2560], BF16)
        xt = [P.palloc("xt%d" % s, [128, 8, TT], F32) for s in range(1)]
        an = P.palloc("an", [128, 8], F32)
        cw = P.palloc("cw", [128, 4, 3], F32)
        self.DMA(an[:], D["a_norm"].ap(), "an", w=["an"])
        self.DMA(cw[:], D["a_conv_w"].ap(), "cw", w=["cw"])
        src = D["a_w_in"].ap().rearrange("(c p) n -> p c n", p=128)
        nst = [0]

        def stgfn():
            k = nst[0] % 2
            nst[0] += 1
            return xt[0][:, :, 256 * k:256 * k + 256], ("xtH", k)

        self.load_seg(Wb, 0, src, 1304, 2560, 8, stgfn, "Wb", chunk=256)
        WbR = [("Wb", c) for c in range(0, 2560, 512)]
        NS = self.alloc_norm_scratch()
        xn2 = [P.palloc("xn%d" % s_, [128, 8, TT], BF16) for s_ in range(2)]
        gast = [P.palloc("gast%d" % s_, [128, TT], BF16) for s_ in range(2)]
        cvst = P.palloc("cvst", [128, 4, TT], BF16)
        U = [P.palloc("U%d" % c, [128, TT + 2], F32) for c in range(4)]
        ccs = P.palloc("ccs", [128, TT], F32)
        yv = P.palloc("yv", [128, TT], F32)
        sg = P.palloc("sg", [128, TT], F32)
        for c in range(4):
            self.MEMSET("pool", U[c][:, 0:2], 0.0, [("Uh", c)])
        xv = D["xT"].ap().rearrange("(c p) t -> p c t", p=128)
        gav = D["gaT"].ap().rearrange("(c p) t -> p c t", p=128)
        mixv = D["mixT"].ap().rearrange("(c p) t -> p c t", p=128)

        def load_x(i):
            sl = 0
            self.DMA(xt[sl][:], xv[:, :, i * TT:(i + 1) * TT], ("xt", sl), w=[("xt", sl), ("xtH", 0), ("xtH", 1)])

        def pre(i):
            load_x(i)
            xtok = [("xt", 0)]
            rstd = self.rstd_tile(xt[0], 8, 1.0 / DM, NS, xtok)
            self.norm_apply(xn2[i % 2], xt[0], 8, an, rstd, xtok, ("xn", i % 2), "an")

        def body(i):
            sl = 0
            tsl = slice(i * TT, (i + 1) * TT)
            xn = xn2[i % 2]
            xnR = [("xn", i % 2, c) for c in range(8)]

            def proj8(ps_ap, col0, m, wtok):
                for kc in range(8):
                    self.MM(ps_ap, Wb[:, kc, col0:col0 + m], xn[:, kc, :], kc == 0, kc == 7, self.wcover("Wb", col0, m) + xnR, wtok)

            for c in range(4):
                b = self.bank(0, 7)
                proj8(self.PS[b][:], 128 * c, 128, [("ps", b)])
                self.ACT(gast[c % 2][:], self.PS[b][:], AF.Silu, [("ps", b)], [("gast", c % 2)])
                self.DMA(gav[:, c, tsl], gast[c % 2][:], ("gast", c % 2), r=[("gast", c % 2)], w=[("gaT_d", i, c)])
            for c in range(4):
                if c == 2:
                    yield
                bcb, bcc, bch, bgb = (self.bank(0, 7) for _ in range(4))
                proj8(self.PS[bcb][:], 512 + 128 * c, 128, [("ps", bcb)])
                if c >= 2:
                    yield
                proj8(self.PS[bcc][:], 1024 + 128 * c, 128, [("ps", bcc)])
                if c >= 2:
                    yield
                proj8(self.PS[bch][:], 1536 + 128 * c, 128, [("ps", bch)])
                if c >= 2:
                    yield
                proj8(self.PS[bgb][:], 2048 + 128 * c, 128, [("ps", bgb)])
                if c >= 2:
                    yield
                self.CP("act", ccs[:], self.PS[bcc][:], [("ps", bcc)], ["ccs"])
                self.TTo("dve", U[c][:, 2:TT + 2], self.PS[bch][:], ccs[:], ALU.mult, [("ps", bch), "ccs"], [("U", c)])
                ur = [("U", c), ("Uh", c), "cw"]
                self.TS("dve", yv[:], U[c][:, 0:TT], cw[:, c, 0:1], None, ALU.mult, None, ur, ["yv"])
                self.STT(yv[:], U[c][:, 1:TT + 1], cw[:, c, 1:2], yv[:], ALU.mult, ALU.add, ur + ["yv"], ["yv"])
                self.STT(yv[:], U[c][:, 2:TT + 2], cw[:, c, 2:3], yv[:], ALU.mult, ALU.add, ur + ["yv"], ["yv"])
                self.CP("pool", U[c][:, 0:2], U[c][:, TT:TT + 2], [("U", c)], [("Uh", c)])
                self.TTo("dve", ccs[:], self.PS[bcb][:], yv[:], ALU.mult, [("ps", bcb), "yv"], ["ccs"])
                self.ACT(sg[:], self.PS[bgb][:], AF.Silu, [("ps", bgb)], ["sg"])
                self.TTo("pool", cvst[:, c, :], ccs[:], sg[:], ALU.mult, ["ccs", "sg"], [("cvst", c)])
            self.DMA(mixv[:, 4:8, tsl], cvst[:], "cvst", r=[("cvst", c) for c in range(4)], w=[("mixc_d", i)])
        pre(0)
        for i in range(NT):
            gen = body(i)
            next(gen)
            aux = []
            if i + 1 < NT:
                P.defer_begin()
                pre(i + 1)
                aux = P.defer_end()
            self.run_interleaved(gen, aux, 8)
        P.phase_end()

    def phase_A2(self):
        P = self.P
        D = self.D
        R = self.R0
        P.phase_begin()
        stg = P.palloc("a2stg", [128, 32, 128], F32)
        W1 = {k: P.palloc("W1" + k, [128, 32, 128], BF16) for k in "kv"}
        W2k = P.palloc("W2k", [128, 128], BF16)
        W2v = P.palloc("W2v", [128, 64], BF16)
        peT = {k: P.palloc("peT" + k, [128, 32], BF16) for k in "kv"}
        s2 = P.palloc("a2s2", [128, 64], F32)
        s3 = P.palloc("a2s3", [128, 32], F32)
        for k, wn, pn in (("k", "a_w_ck1", "a_pe_kT"), ("v", "a_w_cv1", "a_pe_vT")):
            wv = D[wn].ap().rearrange("(l d) h -> d l h", d=64)
            self.DMA(stg[0:64], wv, "a2stg", w=["a2stg"])
            self.DMA(stg[64:128], wv, "a2stg", r=(), w=["a2stg_b"])
            self.CP("dve", W1[k][:], stg[:], ["a2stg", "a2stg_b"], ["W1" + k])
            self.DMA(s3[:], D[pn].ap(), "a2s3", w=["a2s3"])
            self.CP("dve", peT[k][:], s3[:], ["a2s3"], ["peT" + k])
        self.DMA(s2[:], D["a_w_ck2"].ap(), "a2s2", w=["a2s2"])
        self.CP("dve", W2k[:, 0:64], s2[:], ["a2s2"], ["W2k"])
        self.TS("dve", W2k[:, 64:96], s2[:, 32:64], -1.0, None, ALU.mult, None, ["a2s2"], ["W2kra"])
        self.CP("dve", W2k[:, 96:128], s2[:, 0:32], ["a2s2"], ["W2krb"])
        self.DMA(s2[:], D["a_w_cv2"].ap(), "a2s2", w=["a2s2"])
        self.CP("dve", W2v[:], s2[:], ["a2s2"], ["W2v"])
        cst = [P.palloc("cstg%d" % k_, [128, 2048], F32) for k_ in range(2)]
        ev = D["e30k"].ap()
        nc_ = 0
        for c0 in range(0, S, 2048):
            k_ = nc_ % 2
            nc_ += 1
            self.DMA(cst[k_][64:128, :], ev[:, c0:c0 + 2048], ("cstg", k_), w=[("cstg", k_)])
            for g in range(2):
                self.CP("dve" if g == 0 else "pool", R["KE"][g][64:128, c0:c0 + 2048], cst[k_][64:128, :], [("cstg", k_)],
                        [("KE_E", g, c0)])
        tv = D["tbig"].ap()
        for c0 in range(0, S, 2048):
            k_ = nc_ % 2
            nc_ += 1
            self.DMA(cst[k_][:, :], tv[:, c0:c0 + 2048], ("cstg", k_), w=[("cstg", k_)])
            self.TS("dve", R["tbig"][:, c0:c0 + 2048], cst[k_][:, :], 30000.0, -30000.0, ALU.mult, ALU.add, [("cstg", k_)],
                    [("tbig", c0)])
        self.MEMSET("pool", R["Vs"][:, :, :, 64:128], 1.0, ["Vs_ones"])
        self.MEMSET("pool", R["Vw"][:, :, :, 64:128], 1.0, ["Vw_ones"])
        self.MEMSET("pool", R["Vc"][:, :, :, 64:128], 1.0, ["Vc_ones"])
        RS = self.alloc_rope_scratch()
        posfull = P.palloc("posfull", [64, S], I32)
        self.DMA(posfull[:], bass.AP(D["pos"], 0, [[0, 64], [1, S]]), "posfull", w=["posfull"])
        cs, sn, rtoks = self.rope_tables(0, RS, src=posfull[:, 31:31 + 16 * 254 + 1:16], n=255, sb_src=True)
        hb = P.palloc("hb", [128, 1], F32)
        Hs = P.palloc("Hs", [128, 256], BF16)
        for g in range(2):
            self.MEMSET("pool", R["KC"][g][:], 0.0, [("KC", g)])
        for g in range(2):
            rows = slice(64 * g, 64 * g + 64)
            for k in "kv":
                raw = self.KCR["KCR" if k == "k" else "VCR"]
                bH = self.bank(0, 7)
                bB = self.bank(0, 7)
                for l in range(32):
                    self.MM(self.PS[bH][:, 0:255], W1[k][rows, l, :], raw[rows, l:l + 16 * 254 + 1:16], l == 0, l == 31,
                            ["W1" + k], [("ps", bH)])
                for l in range(32):
                    self.MM(self.PS[bB][:, 0:1], W1[k][rows, l, :], peT[k][rows, l:l + 1], l == 0, l == 31,
                            ["W1" + k, "peT" + k], [("ps", bB)])
                self.CP("act", hb[:], self.PS[bB][:, 0:1], [("ps", bB)], ["hb"])
                self.MEMSET("pool", Hs[:, 255:256], 0.0, ["Hs_pad"])
                self.ACT(Hs[:, 0:255], self.PS[bH][:, 0:255], AF.Silu, [("ps", bH), "hb"], ["Hs"], bias=hb[:, 0:1])
                if k == "k":
                    ba = self.bank(0, 7)
                    bb = self.bank(0, 7)
                    self.MM(self.PS[ba][0:64, 0:255], W2k[:, 0:64], Hs[:, 0:255], True, True, ["W2k", "Hs"], [("ps", ba)])
                    self.MM(self.PS[bb][0:64, 0:255], W2k[:, 64:128], Hs[:, 0:255], True, True,
                            ["W2kra", "W2krb", "Hs"], [("ps", bb)])
                    self.rope_apply(R["KC"][g][:, 0:255], self.PS[ba][0:64, 0:255], self.PS[bb][0:64, 0:255], cs, sn, RS,
                                    [("ps", ba), ("ps", bb)] + rtoks, [("KC", g)], n=255)
                else:
                    for ct in range(2):
                        bv = self.bank(0, 7)
                        self.MM(self.PS[bv][:, 0:64], Hs[:, ct * 128:(ct + 1) * 128], W2v[:], True, True,
                                ["Hs", "Hs_pad", "W2v"], [("ps", bv)])
                        self.CP("act", R["Vc"][:, g, ct, 0:64], self.PS[bv][:, 0:64], [("ps", bv)], [("Vc", g, ct)])
        P.phase_end()

    def phase_A2b(self):
        P = self.P
        D = self.D
        R = self.R0
        P.phase_begin()
        mbig = P.palloc("mbig_sb", [128, 512], F32)
        keep = P.palloc("keep_sb", [128, 128], F32)
        addt = P.palloc("addt_sb", [128, 128], F32)
        idf = P.palloc("idf", [128, 128], F32)
        idb = P.palloc("idb", [128, 128], BF16)
        self.DMA(mbig[:], D["mbig"].ap(), "mbig", w=["mbig"])
        self.DMA(keep[:], D["keep"].ap(), "keep", w=["keep"])
        self.DMA(addt[:], D["addt"].ap(), "addt", w=["addt"])
        self.DMA(idf[:], D["ident"].ap(), "idf", w=["idf"])
        self.CP("dve", idb[:], idf[:], ["idf"], ["idb"])
        qt = [P.palloc("qt%d" % s, [64, 8, TT], BF16) for s in range(2)]
        sm = [P.palloc("sm%d" % s, [128, 256], F32) for s in range(2)]
        ex = [P.palloc("ex%d" % s, [128, 256], F32) for s in range(2)]
        mx = [P.palloc("mx%d" % s, [128, 1], F32) for s in range(2)]
        rs = [P.palloc("rs%d" % s, [128, 1], F32) for s in range(2)]
        ri = [P.palloc("ri%d" % s, [128, 1], F32) for s in range(2)]
        acc = P.palloc("acc", [128, 264], F32)
        imp = P.palloc("imp", [128, 64], F32)
        imp2 = P.palloc("imp2", [128, 64], F32)
        imp3 = P.palloc("imp3", [128, 64], F32)
        m8a = P.palloc("m8a", [128, 8], F32)
        m8b = P.palloc("m8b", [128, 8], F32)
        selp = [P.palloc("selp%d" % g, [128, 128], BF16) for g in range(2)]
        for g in range(2):
            self.MEMSET("pool", selp[g][:], 0.0, [("selp", g)])
        q0v = D["q0"].ap().rearrange("h p t -> p h t")
        scale = 0.125
        nh = 0

        def load_q(i):
            sl = i % 2
            self.DMA(qt[sl][:], q0v[:, :, i * TT:(i + 1) * TT], ("qt", sl), w=[("qt", sl)])

        mbb = P.palloc("mbb", [128, 512], BF16)
        self.CP("dve", mbb[:], mbig[:], ["mbig"], ["mbb"])
        rs4 = P.palloc("rs4", [128, 4], F32)
        ri4 = P.palloc("ri4", [128, 4], F32)
        ex4 = [P.palloc("ex4_%d" % s_, [128, 256], F32) for s_ in range(4)]
        G2 = []
        for g in range(4):
            d = dict(
                acc=P.palloc("acc_g%d" % g, [128, 264], F32), imp=P.palloc("imp_g%d" % g, [128, 64], F32),
                imp2=P.palloc("imp2_g%d" % g, [128, 64], F32), imp3=P.palloc("imp3_g%d" % g, [128, 64], F32),
                m8a=P.palloc("m8a_g%d" % g, [128, 8], F32), m8b=P.palloc("m8b_g%d" % g, [128, 8], F32),
                rs4=P.palloc("rs4_g%d" % g, [128, 4], F32), ri4=P.palloc("ri4_g%d" % g, [128, 4], F32),
                ex4=[P.palloc("ex4_g%d_%d" % (g, k), [128, 256], F32) for k in range(4)],
                selp=P.palloc("selp_c%d" % g, [128, 128], BF16))
            self.MEMSET("pool", d["acc"][:], 0.0, [("acc", g)])
            self.MEMSET("pool", d["selp"][:], 0.0, [("selpc", g)])
            G2.append(d)

        def chain(i, g, st, sl):
            c = 2 * (st % 2) + g
            d = G2[c]
            selp_ = d["selp"]
            acc_, imp_, imp2_, imp3_, m8a_, m8b_, rs4_, ri4_, ex4_ = (d[k] for k in
                                                                   ("acc", "imp", "imp2", "imp3", "m8a", "m8b", "rs4", "ri4", "ex4"))
            T = lambda nm: (nm, c)
            sidx = 4 * i + st
            t0 = 128 * sidx
            for hp in range(2):
                b = c
                for hh2 in range(2):
                    h = 4 * g + 2 * hp + hh2
                    self.MM(self.PS[b][:, hh2 * 256:(hh2 + 1) * 256], qt[sl][:, h, st * 128:(st + 1) * 128],
                            R["KC"][g][:, :], True, False, [("qt", sl), ("KC", g)], [("ps", b)])
                    self.MM(self.PS[b][:, hh2 * 256:(hh2 + 1) * 256], self.identb[:],
                            mbb[:, 248 - 8 * sidx:504 - 8 * sidx], False, True, ["identb", "mbb"], [("ps", b)])
                for hh2 in range(2):
                    k = 2 * hp + hh2
                    self.ACT(ex4_[k][:], self.PS[b][:, hh2 * 256:(hh2 + 1) * 256], AF.Exp, [("ps", b)],
                             [("ex4", c, k), ("rs4", c, k)], scale=scale, accum_out=rs4_[:, k:k + 1])
            yield
            rtk = [("rs4", c, k) for k in range(4)]
            self.TS("dve", ri4_[:], rs4_[:], 1e-30, None, ALU.max, None, rtk, [T("ri4")])
            yield
            self.P.dve(lambda h_: h_.reciprocal(out=ri4_[:], in_=ri4_[:]), [T("ri4")], [T("ri4")])
            yield
            self.TS("dve", acc_[:, 1:257], ex4_[0][:], ri4_[:, 0:1], None, ALU.mult, None, [("ex4", c, 0), T("ri4"), T("acc")], [T("acc")])
            yield
            for k in range(1, 4):
                self.STT(acc_[:, 1:257], ex4_[k][:], ri4_[:, k:k + 1], acc_[:, 1:257], ALU.mult, ALU.add,
                         [("ex4", c, k), T("ri4"), T("acc")], [T("acc")])
                yield
            self.TTo("dve", imp_[:], acc_[:, 0:253:4], acc_[:, 1:254:4], ALU.add, [T("acc")], [T("imp")])
            yield
            for j in (2, 3, 4):
                self.TTo("dve", imp_[:], imp_[:], acc_[:, j:j + 253:4], ALU.add, [T("acc"), T("imp")], [T("imp")])
                yield
            self.TTo("dve", imp2_[:], imp_[:], keep[:, 64 - 2 * sidx:128 - 2 * sidx], ALU.mult, [T("imp"), "keep"], [T("imp2")])
            yield
            self.TTo("dve", imp2_[:], imp2_[:], addt[:, 64 - 2 * sidx:128 - 2 * sidx], ALU.add, [T("imp2"), "addt"], [T("imp2")])
            yield
            self.MEMSET("dve", imp2_[:, 0:1], 1.0e4, [T("imp2")])
            yield
            self.P.dve(lambda h_: h_.max(out=m8a_[:], in_=imp2_[:]), [T("imp2")], [T("m8a")])
            yield
            self.P.dve(lambda h_: h_.match_replace(out=imp3_[:], in_to_replace=m8a_[:], in_values=imp2_[:], imm_value=-1.0e9),
                       [T("imp2"), T("m8a")], [T("imp3")])
            yield
            self.P.dve(lambda h_: h_.max(out=m8b_[:], in_=imp3_[:]), [T("imp3")], [T("m8b")])
            yield
            self.TS("dve", selp_[:, 64 * g:64 * g + 64], imp2_[:], m8b_[:, 7:8], 1.0, ALU.is_ge, ALU.subtract,
                    [T("imp2"), T("m8b"), ("selpc", c)], [("selpc", c)])
            bt = 4 + c
            pst = self.PS[bt].bitcast(BF16)
            self.P.pe(lambda h_, o=pst[:, 0:128], a=selp_[:]: h_.transpose(o, a, self.identb[:]),
                      [("selpc", c), "identb"], [("ps", bt)])
            self.CP("act", R["SB"][64 * g:64 * g + 64, t0:t0 + 128], pst[64 * g:64 * g + 64, 0:128],
                    [("ps", bt)], [("SB", g, sidx)])

        cnt2 = [0, 0]
        load_q(0)
        for i in range(NT):
            if i + 1 < NT:
                load_q(i + 1)
            sl = i % 2
            for st0 in range(0, 4, 2):
                gens = [chain(i, g, st0 + u, sl) for u in range(2) for g in range(2)]
                alive = [True] * 4
                while any(alive):
                    for gi in range(4):
                        if alive[gi]:
                            try:
                                next(gens[gi])
                            except StopIteration:
                                alive[gi] = False
        P.phase_end()

    def phase_A3(self):
        P = self.P
        D = self.D
        R = self.R0
        P.phase_begin()
        NQ = 3
        Qs = [P.palloc("Qs%d" % s, [128, TT], BF16) for s in range(NQ)]
        Gbc = [P.palloc("Gbc%d" % s, [64, 3, TT], F32) for s in range(3)]
        gat = [P.palloc("gat%d" % s, [64, TT], BF16) for s in range(3)]
        NPT = 6
        pt = [P.palloc("pt%d" % s, [128, TT], BF16) for s in range(NPT)]
        lsb = [P.palloc("lsb%d" % s, [64, TT], F32) for s in range(3)]
        wb = [P.palloc("wb%d" % s, [64, TT], F32) for s in range(3)]
        ob = [P.palloc("ob%d" % s, [64, TT], F32) for s in range(3)]
        mst = [P.palloc("mst%d" % s, [64, TT], BF16) for s in range(2)]
        scale = 0.125
        cnt = dict(pt=0, q=0, sb=0, ob=0, mk=0, ev=0)
        gTa_t = D["gTa"]

        def load_q(h, i):
            qs = cnt["q"] % NQ
            cnt["q"] += 1
            g = h // 4
            tsl = slice(i * TT, (i + 1) * TT)
            self.DMA(Qs[qs][0:64, :], D["q0"].ap()[h][:, tsl], ("Qs", qs), w=[("Qs", qs)])
            self.DMA(Qs[qs][64:128, :], R["SB"][64 * g:64 * g + 64, tsl], ("QsB", qs), w=[("QsB", qs)])
            e2 = (cnt["q"] - 1) % 3
            src = bass.AP(gTa_t, 3 * h * S + i * TT, [[0, 64], [S, 3], [1, TT]])
            self.DMA(Gbc[e2][:], src, ("Gbc", e2), w=[("Gbc", e2)])
            self.DMA(gat[e2][:], D["gaT"].ap()[64 * h:64 * h + 64, tsl], ("gat", e2), w=[("gat", e2)])
            return qs, e2

        eps18 = P.palloc("eps18", [128, 1], F32)
        self.MEMSET("dve", eps18[:], 1e-18, ["eps18"])
        seq = [(h, i) for h in range(8) for i in range(NT)]
        slots = {seq[0]: load_q(*seq[0])}
        stream = []
        for n, (h, i) in enumerate(seq):
            g = h // 4
            tl = []
            cts = [0] + ([1] if i >= 4 else [])
            for n_, ct in enumerate(cts):
                off = 512 * i - 2048 * ct
                m_ap = None if off >= 2063 else R["tbig"][:, off:off + TT]
                tl.append(dict(br=0, lhsT=R["KC"][g][:, ct * 128:(ct + 1) * 128], full=False, ktok=[("KC", g)], mask=m_ap,
                               cols=(0, TT), mcols=(0, TT),
                               V=R["Vc"][:, g, ct, :], vtok=[("Vc", g, ct), "Vc_ones"], first=n_ == 0, last=n_ == len(cts) - 1))
            nk = 4 * i + 4
            for kt in range(nk):
                ksl = slice(kt * 128, (kt + 1) * 128)
                m_ap = None
                cols = (0, TT)
                mcols = None
                if kt >= 4 * i:
                    j = kt - 4 * i
                    m_ap = self.cbig[:, 384:512]
                    cols = (128 * j, TT)
                    mcols = (128 * j, 128 * j + 128)
                tl.append(dict(br=1, lhsT=R["KE"][g][:, ksl], full=True, ktok=[("KE", g)], mask=m_ap, cols=cols, mcols=mcols,
                               V=R["Vs"][:, g, kt, :], vtok=[("Vs", g)], first=kt == 0, last=kt == nk - 1))
            k0 = max(0, 4 * i - 4)
            wkts = [4 * i] + [kt for kt in range(k0, nk) if kt != 4 * i]
            for wn, kt in enumerate(wkts):
                ksl = slice(kt * 128, (kt + 1) * 128)
                if kt >= 4 * i:
                    j = kt - 4 * i
                    m_ap = self.cbig[:, 384:512]
                    cols = (128 * j, TT)
                else:
                    j = kt - (4 * i - 4)
                    m_ap = self.wbig[:, 384:512]
                    cols = (0, 128 * j + 128)
                mcols = (128 * j, 128 * j + 128)
                tl.append(dict(br=2, lhsT=R["KW"][g][:, ksl], full=False, ktok=[("KW", g)], mask=m_ap, cols=cols, mcols=mcols,
                               V=R["Vw"][:, g, kt, :], vtok=[("Vw", g)], first=wn == 0, last=wn == len(wkts) - 1))
            for ti, t in enumerate(tl):
                t.update(n=n, h=h, i=i, gfirst=ti == 0, glast=ti == len(tl) - 1)
                stream.append(t)
        state = {}

        def front(t):
            n, h, i = t["n"], t["h"], t["i"]
            if t["gfirst"]:
                if n + 1 < len(seq):
                    slots[seq[n + 1]] = load_q(*seq[n + 1])
                bOs = []
                for _ in range(3):
                    bOs.append(3 + cnt["ob"] % 5)
                    cnt["ob"] += 1
                state[n] = dict(bOs=bOs)
            qs, e2 = slots[(h, i)]
            bS = cnt["sb"] % 3
            cnt["sb"] += 1
            c0, c1 = t["cols"]
            rhs = Qs[qs][:, c0:c1] if t["full"] else Qs[qs][0:64, c0:c1]
            msk = t["mask"] is not None
            self.MM(self.PS[bS][:, c0:c1], t["lhsT"], rhs, True, not msk, t["ktok"] + [("Qs", qs), ("QsB", qs)], [("ps", bS)])
            if msk:
                m0, m1 = t["mcols"]
                self.MM(self.PS[bS][:, m0:m1], self.identb[:], t["mask"], False, True, ["identb", "cbig", "wbig"], [("ps", bS)])
            p_ = cnt["pt"] % NPT
            cnt["pt"] += 1
            self.ACT(pt[p_][:, c0:c1], self.PS[bS][:, c0:c1], AF.Exp, [("ps", bS)], [("pt", p_)], scale=scale)
            t["p_"] = p_

        def back(t):
            n, h, i = t["n"], t["h"], t["i"]
            p_ = t["p_"]
            bO = state[n]["bOs"][t["br"]]
            c0, c1 = t["cols"]
            self.MM(self.PS[bO][:, c0:c1], t["V"], pt[p_][:, c0:c1], t["first"], t["last"], t["vtok"] + [("pt", p_)], [("ps", bO)])
            if not t["glast"]:
                return
            P.defer_begin()
            qs, e2 = slots[(h, i)]
            tsl = slice(i * TT, (i + 1) * TT)
            for bi, bO in enumerate(state[n]["bOs"]):
                if bi == 0:
                    self.ACT(lsb[bi][:], self.PS[bO][64:128, :], AF.Ln, [("ps", bO), "eps18"], [("lsb", bi)], bias=eps18[0:64, 0:1])
                else:
                    self.ACT(lsb[bi][:], self.PS[bO][64:128, :], AF.Ln, [("ps", bO)], [("lsb", bi)])
                self.ACT(wb[bi][:], lsb[bi][:], AF.Exp, [("lsb", bi)], [("wb", bi)], scale=-1.0)
                self.TTo("pool", wb[bi][:], wb[bi][:], Gbc[e2][:, bi, :], ALU.mult, [("wb", bi), ("Gbc", e2)], [("wb", bi)])
                self.TTo("dve", ob[bi][:], self.PS[bO][0:64, :], wb[bi][:], ALU.mult, [("ps", bO), ("wb", bi)], [("ob", bi)])
            self.TTo("pool", ob[0][:], ob[0][:], ob[1][:], ALU.add, [("ob", 0), ("ob", 1)], [("ob", 0)])
            self.TTo("pool", ob[0][:], ob[0][:], ob[2][:], ALU.add, [("ob", 0), ("ob", 2)], [("ob", 0)])
            ms = n % 2
            self.TTo("pool", mst[ms][:], ob[0][:], gat[e2][:], ALU.mult, [("ob", 0), ("gat", e2)], [("mst", ms)])
            self.DMA(D["mixT"].ap()[64 * h:64 * h + 64, tsl], mst[ms][:], ("mst", ms), r=[("mst", ms)], w=[("mix_d", h, i)])
            epq.extend(P.defer_end())

        LA = 2
        epq = []
        for idx in range(len(stream) + LA):
            if idx < len(stream):
                front(stream[idx])
            if idx - LA >= 0:
                back(stream[idx - LA])
            if epq:
                P.replay(epq[:2])
                del epq[:2]
        P.replay(epq)
        P.phase_end()

    def phase_A4(self):
        P = self.P
        D = self.D
        P.phase_begin()
        self._wst = 0
        Wo = P.palloc("Woa", [128, 8, 1024], BF16)
        xt = [P.palloc("xt%d" % s, [128, 8, TT], F32) for s in range(2)]
        mx = [P.palloc("mixs%d" % s, [128, 8, TT], BF16) for s in range(2)]
        x1 = P.palloc("x1o", [128, 8, TT], F32)
        nst = [0]

        wstg = [P.palloc("a4stg%d" % s_, [128, 8, 256], F32) for s_ in range(2)]

        def stgfn():
            sl = nst[0] % 2
            nst[0] += 1
            return wstg[sl], ("a4stg", sl)

        WoR = [("Woa", 0), ("Woa", 512)]
        xv = D["xT"].ap().rearrange("(c p) t -> p c t", p=128)
        mixv = D["mixT"].ap().rearrange("(c p) t -> p c t", p=128)
        x1v = D["x1T"].ap().rearrange("(c p) t -> p c t", p=128)

        def load(i):
            sl = i % 2
            tsl = slice(i * TT, (i + 1) * TT)
            self.DMA(xt[sl][:], xv[:, :, tsl], ("xt", sl), w=[("xt", sl)])
            self.DMA(mx[sl][:], mixv[:, :, tsl], ("mixs", sl), w=[("mixs", sl)])

        load(0)
        for c0_ in range(0, 1024, 256):
            self.load_seg(Wo, c0_, D["a_w_out"].ap().rearrange("(c p) n -> p c n", p=128), c0_, 256, 8, stgfn, "Woa")
        for i in range(NT):
            if i + 1 < NT:
                load(i + 1)
            sl = i % 2
            tsl = slice(i * TT, (i + 1) * TT)
            for dc in range(8):
                b = self.bank(0, 8)
                for kc in range(8):
                    self.MM(self.PS[b][:], Wo[:, kc, dc * 128:(dc + 1) * 128], mx[sl][:, kc, :], kc == 0, kc == 7,
                            self.wcover("Woa", dc * 128, 128) + [("mixs", sl)], [("ps", b)])
                self.TTo("dve", x1[:, dc, :], self.PS[b][:], xt[sl][:, dc, :], ALU.add, [("ps", b), ("xt", sl)], [("x1o", dc)])
            self.DMA(x1v[:, :, tsl], x1[:], "x1o", r=[("x1o", dc) for dc in range(8)], w=[("x1_d", i)])
        P.phase_end()

    def declare_io(self):
        mode = self.mode
        self.din("pos", [1, S], I32)
        self.din("invf", [128, 1])
        self.din("cbig", [128, 896])
        self.din("ident", [128, 128])
        if mode in ("full", "L0"):
            self.din("xT", [DM, S])
            self.din("a_norm", [128, 8])
            self.din("a_w_in", [DM, 3864])
            self.din("a_pe_kT", [128, 32])
            self.din("a_pe_vT", [128, 32])
            self.din("a_w_ck1", [2048, 128])
            self.din("a_w_ck2", [128, 64])
            self.din("a_w_cv1", [2048, 128])
            self.din("a_w_cv2", [128, 64])
            self.din("a_conv_w", [128, 4, 3])
            self.din("a_w_out", [DM, DM])
            self.din("e30k", [64, S])
            self.din("tbig", [128, S])
            self.din("mbig", [128, 512])
            self.din("keep", [128, 128])
            self.din("addt", [128, 128])
            for nm, shp, dt in (("q0", [8, 64, S], BF16), ("gTa", [24, S], F32), ("gaT", [512, S], BF16),
                                ("mixT", [DM, S], BF16)):
                self.dscr(nm, shp, dt)
        if mode in ("full", "L1"):
            self.din("c_norm", [128, 8])
            self.din("c_w_in", [DM, 1600])
            self.din("c_q_norm", [128, 2])
            self.din("c_kv_norm", [128, 2])
            self.din("c_w_uq", [256, 1536])
            self.din("c_w_ukv", [256, 2048])
            self.din("c_w_out", [DM, DM])
            self.din("final_norm", [128, 8])
            self.dout("outT", [DM, S])
            for nm, shp, dt in (("gT1", [DM, S], BF16), ("qn", [8, 128, S], BF16), ("qr", [8, 64, S], BF16),
                                ("kn", [8, 128, S], BF16), ("kpe", [64, S], BF16), ("vS", [8, S, 128], BF16),
                                ("oT", [DM, S], BF16)):
                self.dscr(nm, shp, dt)
        if mode == "L1":
            self.din("x1T", [DM, S])
        elif mode == "full":
            self.dscr("x1T", [DM, S], F32)
        else:
            self.dout("x1T", [DM, S])

    def build(self):
        self.declare_io()
        self.setup_globals()
        if self.mode in ("full", "L0"):
            self.layer0()
        if self.mode in ("full", "L1"):
            for nm in ("C1", "C2", "C3"):
                getattr(self, "phase_" + nm)()
                if self.stop_after == nm:
                    break
        self.P.barrier()
        self.P.emit()
        self.P.close()
        return self.nc


def _vec(v, k):
    return np.ascontiguousarray(np.asarray(v, np.float32).reshape(k, 128).T)


def const_inputs():
    half = 32
    invf32 = (np.float32(10000.0) ** (-(np.arange(half, dtype=np.float32) / np.float32(half)))).astype(np.float32)
    invf = np.concatenate([invf32, invf32, invf32, invf32])[:, None].astype(np.float32)
    kk = np.arange(128)[:, None]
    w = np.arange(896)[None, :]
    cbig = (kk <= (w - 384)).astype(np.float32)
    return {"invf": invf, "cbig": cbig, "ident": np.eye(128, dtype=np.float32)}


def layer1_inputs(c_norm, c_w_in, c_q_norm, c_kv_norm, c_w_uq, c_w_ukv, c_w_out, final_norm):
    return {
        "c_norm": _vec(c_norm[0], 8), "c_w_in": np.ascontiguousarray(c_w_in[0], np.float32),
        "c_q_norm": _vec(c_q_norm[0], 2), "c_kv_norm": _vec(c_kv_norm[0], 2),
        "c_w_uq": np.ascontiguousarray(c_w_uq[0], np.float32), "c_w_ukv": np.ascontiguousarray(c_w_ukv[0], np.float32),
        "c_w_out": np.ascontiguousarray(c_w_out[0], np.float32), "final_norm": _vec(final_norm, 8),
    }


def const_inputs0():
    c = {}
    n = np.arange(64)[:, None]
    key = np.arange(S)[None, :]
    c["e30k"] = ((key // 64) == n).astype(np.float32) * np.float32(30000.0)
    cc = np.arange(128)[:, None]
    w = np.arange(S)[None, :]
    c["tbig"] = ((16 * cc + 31) <= w).astype(np.float32)
    tl = np.arange(128)[:, None]
    u = np.arange(512)[None, :]
    c["mbig"] = np.where((16 * (u - 248) + 31) <= tl, 0.0, NEG).astype(np.float32)
    u = np.arange(128)[None, :]
    q = tl // 64
    c["keep"] = ((u - 64) <= (q - 2)).astype(np.float32)
    addt = np.zeros((128, 128), np.float32)
    addt[((u - 64) == q) | ((u - 64) == (q - 1))] = 1.0e4
    addt[(u - 64) > q] = -1.0
    c["addt"] = addt
    return c


def layer0_inputs(a_norm, a_w_in, a_pe_k, a_pe_v, a_w_ck1, a_w_ck2, a_w_cv1, a_w_cv2, a_conv_w, a_w_out):
    f = lambda a: np.ascontiguousarray(a, np.float32)
    pekT = f(a_pe_k[0]).T
    pevT = f(a_pe_v[0]).T
    return {
        "a_norm": _vec(a_norm[0], 8), "a_w_in": f(a_w_in[0]),
        "a_pe_kT": f(np.concatenate([pekT, pekT], 0)), "a_pe_vT": f(np.concatenate([pevT, pevT], 0)),
        "a_w_ck1": f(a_w_ck1[0]), "a_w_ck2": f(a_w_ck2[0]), "a_w_cv1": f(a_w_cv1[0]), "a_w_cv2": f(a_w_cv2[0]),
        "a_conv_w": f(f(a_conv_w[0]).T.reshape(4, 128, 3).transpose(1, 0, 2)), "a_w_out": f(a_w_out[0]),
    }


_NC_CACHE = {}


def _get_nc():
    if "nc" not in _NC_CACHE:
        _NC_CACHE["nc"] = Builder("full").build()
    return _NC_CACHE["nc"]


def kernel(x, positions, a_norm, a_w_in, a_pe_k, a_pe_v, a_w_ck1, a_w_ck2, a_w_cv1, a_w_cv2,
           a_conv_w, a_w_out, c_norm, c_w_in, c_q_norm, c_kv_norm, c_w_uq, c_w_ukv, c_w_out,
           final_norm):
    x = np.asarray(x, np.float32)
    positions = np.asarray(positions).astype(np.int32)
    nb = x.shape[0]
    shared = dict(const_inputs())
    shared.update(const_inputs0())
    shared.update(layer0_inputs(np.asarray(a_norm), np.asarray(a_w_in), np.asarray(a_pe_k), np.asarray(a_pe_v),
                                np.asarray(a_w_ck1), np.asarray(a_w_ck2), np.asarray(a_w_cv1), np.asarray(a_w_cv2),
                                np.asarray(a_conv_w), np.asarray(a_w_out)))
    shared.update(layer1_inputs(np.asarray(c_norm), np.asarray(c_w_in), np.asarray(c_q_norm), np.asarray(c_kv_norm),
                                np.asarray(c_w_uq), np.asarray(c_w_ukv), np.asarray(c_w_out), np.asarray(final_norm)))
    in_maps = []
    for b in range(nb):
        m = dict(shared)
        m["xT"] = np.ascontiguousarray(x[b].T)
        m["pos"] = np.ascontiguousarray(positions[b:b + 1])
        in_maps.append(m)
    nc = _get_nc()
    res = run_bass_kernel_spmd(nc, in_maps, core_ids=list(range(nb)))
    out = np.stack([np.ascontiguousarray(r["outT"].T) for r in res.results], axis=0)
    return out.astype(np.float32)
```
